# Optimizing a Trainium2 kernel written in Bass

```python
import math
import jax, jax.numpy as jnp
from jax import lax
import numpy as np

D_MODEL = 1024
BATCH = 2
SEQ = 8192
DEPTH = 1
DEC_BATCH = 128
DEC_SEQ = 8
PAST_LEN = 16384
PAGE_SIZE = 128

DN_HEADS = 4
DN_DK = 128
DN_DV = 128
DN_CONV = 4
DN_CHUNK = 64
SWA_HEADS = 8
SWA_KV_HEADS = 2
SWA_GROUP = SWA_HEADS // SWA_KV_HEADS
SWA_HD = 64
WINDOW = 128
SWA_BLOCK = 128
N_BUCKETS = 32
MAX_DISTANCE = 128
D_FF = 2816
FFN_CONV = 3
EPS = 1e-6
NEG_INF = -1e30

DN_QK = DN_HEADS * DN_DK
DN_V = DN_HEADS * DN_DV
DN_CONV_CH = 2 * DN_QK + DN_V
SWA_Q = SWA_HEADS * SWA_HD
SWA_KV = SWA_KV_HEADS * SWA_HD
D_MIX = DN_V + SWA_Q
IN_SPLITS = (DN_CONV_CH, DN_V, DN_HEADS, DN_HEADS, SWA_Q, SWA_KV, SWA_KV)
IN_COLS = sum(IN_SPLITS)

kernel_name = 'hymba_gdn_swa_convffn_step'


def _split_cols(t, sizes):
    idx = [int(i) for i in np.cumsum(sizes)[:-1]]
    return jnp.split(t, idx, axis=-1)


def _rmsnorm(x, w):
    xf = x.astype(jnp.float32)
    y = xf * lax.rsqrt(jnp.mean(xf * xf, axis=-1, keepdims=True) + EPS)
    return (y * w.astype(jnp.float32)).astype(x.dtype)


def _l2norm(t):
    t = t.astype(jnp.float32)
    return t * lax.rsqrt(jnp.sum(t * t, axis=-1, keepdims=True) + EPS)


def _causal_dwconv(x, buf, w):
    width = w.shape[0]
    L = x.shape[1]
    xp = jnp.concatenate([buf.astype(x.dtype), x], axis=1)
    out = xp[:, 0:L] * w[0]
    for j in range(1, width):
        out = out + xp[:, j:j + L] * w[j]
    return out, xp[:, L:]


def _t5_bucket(dist):
    d = jnp.maximum(dist, 0)
    exact = N_BUCKETS // 2
    logv = jnp.log(jnp.maximum(d, 1).astype(jnp.float32) / exact) / math.log(MAX_DISTANCE / exact)
    large = jnp.minimum(exact + (logv * (N_BUCKETS - exact)).astype(jnp.int32), N_BUCKETS - 1)
    return jnp.where(d < exact, d, large)


def _gated_delta_chunked(q, k, v, g, beta, S0):
    B, L, H, dk = q.shape
    dv = v.shape[-1]
    C = min(DN_CHUNK, L)
    pad = (-L) % C
    if pad:
        pw = lambda t: jnp.pad(t, [(0, 0), (0, pad)] + [(0, 0)] * (t.ndim - 2))
        q, k, v, g, beta = pw(q), pw(k), pw(v), pw(g), pw(beta)
    NC = (L + pad) // C

    def chunks(t):
        return jnp.moveaxis(t.reshape((B, NC, C) + t.shape[2:]), 3, 1)

    qc, kc, vc, gc, bc = chunks(q), chunks(k), chunks(v), chunks(g), chunks(beta)
    G = jnp.cumsum(gc, axis=-1)
    incl = jnp.tril(jnp.ones((C, C), bool))
    strict = jnp.tril(jnp.ones((C, C), bool), -1)
    decay = jnp.where(incl, jnp.exp(jnp.where(incl, G[..., :, None] - G[..., None, :], 0.0)), 0.0)
    kk = jnp.einsum('bhnid,bhnjd->bhnij', kc, kc)
    A = jnp.where(strict, bc[..., :, None] * kk * decay, 0.0)
    eye = jnp.eye(C, dtype=jnp.float32)
    T = lax.linalg.triangular_solve(A + eye, jnp.broadcast_to(eye, A.shape),
                                    left_side=True, lower=True, unit_diagonal=True)
    eG = jnp.exp(G)
    w_v = jnp.einsum('bhnij,bhnjd->bhnid', T, vc * bc[..., None])
    w_k = jnp.einsum('bhnij,bhnjd->bhnid', T, kc * (bc * eG)[..., None])
    qk = jnp.einsum('bhnid,bhnjd->bhnij', qc, kc) * decay
    q_dec = qc * eG[..., None]
    k_tail = kc * jnp.exp(G[..., -1:] - G)[..., None]
    c_dec = jnp.exp(G[..., -1])
    xs = tuple(jnp.moveaxis(t, 2, 0) for t in (w_v, w_k, qk, q_dec, k_tail, c_dec))

    def step(S, inp):
        wv, wk, qk_i, qd, kt, cd = inp
        u = wv - jnp.einsum('bhik,bhkv->bhiv', wk, S)
        o = jnp.einsum('bhik,bhkv->bhiv', qd, S) + jnp.einsum('bhij,bhjv->bhiv', qk_i, u)
        S = S * cd[..., None, None] + jnp.einsum('bhik,bhiv->bhkv', kt, u)
        return S, o

    S_fin, o = lax.scan(step, S0, xs)
    o = jnp.moveaxis(jnp.moveaxis(o, 0, 2), 1, 3).reshape(B, NC * C, H, dv)[:, :L]
    return o, S_fin


def _deltanet_mixer(qkv, z, b_raw, a_raw, conv_buf, S0, conv_w, A_log, dt_bias, norm_w):
    B, L, _ = qkv.shape
    c_out, new_buf = _causal_dwconv(qkv, conv_buf, conv_w)
    c_out = jax.nn.silu(c_out)
    q, k, v = jnp.split(c_out, [DN_QK, 2 * DN_QK], axis=-1)
    q = _l2norm(q.reshape(B, L, DN_HEADS, DN_DK)) * (DN_DK ** -0.5)
    k = _l2norm(k.reshape(B, L, DN_HEADS, DN_DK))
    v = v.reshape(B, L, DN_HEADS, DN_DV).astype(jnp.float32)
    beta = jax.nn.sigmoid(b_raw.astype(jnp.float32))
    g = -jnp.exp(A_log.astype(jnp.float32)) * jax.nn.softplus(a_raw.astype(jnp.float32) + dt_bias.astype(jnp.float32))
    o, S = _gated_delta_chunked(q, k, v, g, beta, S0.astype(jnp.float32))
    o = _rmsnorm(o, norm_w) * jax.nn.silu(z.reshape(B, L, DN_HEADS, DN_DV).astype(jnp.float32))
    return o.reshape(B, L, DN_V).astype(qkv.dtype), new_buf, S.astype(S0.dtype)


def _swa_attend(q, k, v, q_pos, k_pos, sinks, rel_bias):
    s = jnp.einsum('bnihgd,bnjhd->bnhgij', q, k).astype(jnp.float32) * (SWA_HD ** -0.5)
    dist = q_pos[:, :, None] - k_pos[:, None, :]
    valid = (dist >= 0) & (dist < WINDOW) & (k_pos[:, None, :] >= 0)
    nb, lq, lk = dist.shape
    bias = rel_bias.astype(jnp.float32)[_t5_bucket(dist)]
    bias = jnp.transpose(bias.reshape(nb, lq, lk, SWA_KV_HEADS, SWA_GROUP), (0, 3, 4, 1, 2))
    s = jnp.where(valid[:, None, None], s + bias, NEG_INF)
    sink = sinks.astype(jnp.float32).reshape(SWA_KV_HEADS, SWA_GROUP)[:, :, None, None]
    m = jnp.maximum(jnp.max(s, axis=-1, keepdims=True), sink)
    p = jnp.exp(s - m)
    p = p / (jnp.sum(p, axis=-1, keepdims=True) + jnp.exp(sink - m))
    return jnp.einsum('bnhgij,bnjhd->bnihgd', p.astype(v.dtype), v)


def _swa_prompt(q, k, v, sinks, rel_bias):
    B, L = q.shape[:2]
    NB = L // SWA_BLOCK
    qb = q.reshape(B, NB, SWA_BLOCK, SWA_KV_HEADS, SWA_GROUP, SWA_HD)
    kb = k.reshape(B, NB, SWA_BLOCK, SWA_KV_HEADS, SWA_HD)
    vb = v.reshape(B, NB, SWA_BLOCK, SWA_KV_HEADS, SWA_HD)

    def with_prev(t):
        prev = jnp.concatenate([jnp.zeros_like(t[:, :1]), t[:, :-1]], axis=1)
        return jnp.concatenate([prev, t], axis=2)

    blk = jnp.arange(NB, dtype=jnp.int32)[:, None] * SWA_BLOCK
    q_pos = blk + jnp.arange(SWA_BLOCK, dtype=jnp.int32)[None]
    k_pos = blk + jnp.arange(-SWA_BLOCK, SWA_BLOCK, dtype=jnp.int32)[None]
    o = _swa_attend(qb, with_prev(kb), with_prev(vb), q_pos, k_pos, sinks, rel_bias)
    n_buf = min(WINDOW, L)
    return o.reshape(B, L, SWA_Q), k[:, L - n_buf:], v[:, L - n_buf:]


def _swa_sample(q, k, v, k_buf, v_buf, sinks, rel_bias):
    B, L = q.shape[:2]
    n_buf = k_buf.shape[1]
    keys = jnp.concatenate([k_buf.astype(k.dtype), k], axis=1)
    vals = jnp.concatenate([v_buf.astype(v.dtype), v], axis=1)
    q_pos = (PAST_LEN + jnp.arange(L, dtype=jnp.int32))[None]
    k_pos = (PAST_LEN - n_buf + jnp.arange(n_buf + L, dtype=jnp.int32))[None]
    o = _swa_attend(q[:, None], keys[:, None], vals[:, None], q_pos, k_pos, sinks, rel_bias)
    return o.reshape(B, L, SWA_Q), keys[:, L:], vals[:, L:]


def _trunk(x, c, states, layer_w, rel_bias, final_norm_w):
    (w_ada, b_ada, norm_mix_w, w_in, dn_conv_w, dn_A_log, dn_dt_bias, dn_norm_w,
     swa_sinks, w_out, norm_ffn_w, ffn_w_up, ffn_conv_w, ffn_conv_b, ffn_w_down) = layer_w
    B, L, _ = x.shape
    is_sample = states is not None
    outs = ([], [], [], [], [])
    for i in range(DEPTH):
        mod = jnp.einsum('bd,de->be', jax.nn.silu(c), w_ada[i]) + b_ada[i]
        sh1, sc1, g1, sh2, sc2, g2 = jnp.split(mod[:, None, :], 6, axis=-1)
        h = _rmsnorm(x, norm_mix_w[i]) * (1 + sc1) + sh1
        proj = jnp.einsum('bld,de->ble', h, w_in[i])
        qkv, z, b_raw, a_raw, sq, sk, sv = _split_cols(proj, IN_SPLITS)
        if is_sample:
            dn_buf, S0, k_buf, v_buf, f_buf = (s[i] for s in states)
        else:
            dn_buf = jnp.zeros((B, DN_CONV - 1, DN_CONV_CH), x.dtype)
            S0 = jnp.zeros((B, DN_HEADS, DN_DK, DN_DV), jnp.float32)
            f_buf = jnp.zeros((B, FFN_CONV - 1, 2 * D_FF), x.dtype)
        o_dn, n_dn_buf, n_S = _deltanet_mixer(qkv, z, b_raw, a_raw, dn_buf, S0, dn_conv_w[i],
                                              dn_A_log[i], dn_dt_bias[i], dn_norm_w[i])
        sq = sq.reshape(B, L, SWA_KV_HEADS, SWA_GROUP, SWA_HD)
        sk = sk.reshape(B, L, SWA_KV_HEADS, SWA_HD)
        sv = sv.reshape(B, L, SWA_KV_HEADS, SWA_HD)
        if is_sample:
            o_swa, n_k, n_v = _swa_sample(sq, sk, sv, k_buf, v_buf, swa_sinks[i], rel_bias)
        else:
            o_swa, n_k, n_v = _swa_prompt(sq, sk, sv, swa_sinks[i], rel_bias)
        mix = jnp.concatenate([o_dn, o_swa.astype(o_dn.dtype)], axis=-1)
        x = x + g1 * jnp.einsum('ble,ed->bld', mix, w_out[i])
        h = _rmsnorm(x, norm_ffn_w[i]) * (1 + sc2) + sh2
        u = jnp.einsum('bld,df->blf', h, ffn_w_up[i])
        u_c, n_f_buf = _causal_dwconv(u, f_buf, ffn_conv_w[i])
        gate, up = jnp.split(u_c + ffn_conv_b[i], 2, axis=-1)
        x = x + g2 * jnp.einsum('blf,fd->bld', jax.nn.silu(gate) * up, ffn_w_down[i])
        for lst, val in zip(outs, (n_dn_buf, n_S, n_k, n_v, n_f_buf)):
            lst.append(val)
    y = _rmsnorm(x, final_norm_w)
    return y, [jnp.stack(lst) for lst in outs]


def setup_inputs(seed: int = 0) -> dict:
    key = jax.random.key(seed)
    ks = jax.random.split(key, 32)
    f32 = jnp.float32
    nrm = lambda k, shape, s: jax.random.normal(k, shape, f32) * s
    n_buf = min(WINDOW, PAST_LEN)
    dt = jnp.exp(jax.random.uniform(ks[20], (DEPTH, DN_HEADS), f32, math.log(1e-3), math.log(1e-1)))
    return {
        'x_prompt': nrm(ks[0], (BATCH, SEQ, D_MODEL), 1.0),
        'x_sample': nrm(ks[1], (DEC_BATCH, DEC_SEQ, D_MODEL), 1.0),
        'state_dn_conv': nrm(ks[2], (DEPTH, DEC_BATCH, DN_CONV - 1, DN_CONV_CH), 1.0),
        'state_dn_ssm': nrm(ks[3], (DEPTH, DEC_BATCH, DN_HEADS, DN_DK, DN_DV), DN_DK ** -0.5),
        'cache_swa_k': nrm(ks[4], (DEPTH, DEC_BATCH, n_buf, SWA_KV_HEADS, SWA_HD), 1.0),
        'cache_swa_v': nrm(ks[5], (DEPTH, DEC_BATCH, n_buf, SWA_KV_HEADS, SWA_HD), 1.0),
        'state_ffn_conv': nrm(ks[6], (DEPTH, DEC_BATCH, FFN_CONV - 1, 2 * D_FF), 1.0),
        'c_prompt': nrm(ks[7], (BATCH, D_MODEL), 1.0),
        'c_sample': nrm(ks[8], (DEC_BATCH, D_MODEL), 1.0),
        'rel_bias': nrm(ks[9], (N_BUCKETS, SWA_HEADS), 0.5),
        'final_norm_w': 1.0 + nrm(ks[10], (D_MODEL,), 0.05),
        'w_ada': nrm(ks[11], (DEPTH, D_MODEL, 6 * D_MODEL), 0.5 * D_MODEL ** -0.5),
        'b_ada': nrm(ks[12], (DEPTH, 6 * D_MODEL), 0.02),
        'norm_mix_w': 1.0 + nrm(ks[13], (DEPTH, D_MODEL), 0.05),
        'w_in': nrm(ks[14], (DEPTH, D_MODEL, IN_COLS), D_MODEL ** -0.5),
        'dn_conv_w': nrm(ks[15], (DEPTH, DN_CONV, DN_CONV_CH), DN_CONV ** -0.5),
        'dn_A_log': jnp.log(jax.random.uniform(ks[16], (DEPTH, DN_HEADS), f32, 1.0, 16.0)),
        'dn_dt_bias': dt + jnp.log(-jnp.expm1(-dt)),
        'dn_norm_w': 1.0 + nrm(ks[17], (DEPTH, DN_DV), 0.05),
        'swa_sinks': nrm(ks[18], (DEPTH, SWA_HEADS), 0.5),
        'w_out': nrm(ks[19], (DEPTH, D_MIX, D_MODEL), D_MIX ** -0.5),
        'norm_ffn_w': 1.0 + nrm(ks[21], (DEPTH, D_MODEL), 0.05),
        'ffn_w_up': nrm(ks[22], (DEPTH, D_MODEL, 2 * D_FF), D_MODEL ** -0.5),
        'ffn_conv_w': nrm(ks[23], (DEPTH, FFN_CONV, 2 * D_FF), FFN_CONV ** -0.5),
        'ffn_conv_b': nrm(ks[24], (DEPTH, 2 * D_FF), 0.02),
        'ffn_w_down': nrm(ks[25], (DEPTH, D_FF, D_MODEL), D_FF ** -0.5),
    }


def reference(x_prompt, x_sample, state_dn_conv, state_dn_ssm, cache_swa_k, cache_swa_v, state_ffn_conv,
              c_prompt, c_sample, rel_bias, final_norm_w, w_ada, b_ada, norm_mix_w, w_in, dn_conv_w,
              dn_A_log, dn_dt_bias, dn_norm_w, swa_sinks, w_out, norm_ffn_w, ffn_w_up, ffn_conv_w,
              ffn_conv_b, ffn_w_down):
    layer_w = (w_ada, b_ada, norm_mix_w, w_in, dn_conv_w, dn_A_log, dn_dt_bias, dn_norm_w,
               swa_sinks, w_out, norm_ffn_w, ffn_w_up, ffn_conv_w, ffn_conv_b, ffn_w_down)
    y_prompt, (p_dn_conv, p_dn_ssm, p_swa_k, p_swa_v, p_ffn_conv) = _trunk(
        x_prompt, c_prompt, None, layer_w, rel_bias, final_norm_w)
    sample_states = (state_dn_conv, state_dn_ssm, cache_swa_k, cache_swa_v, state_ffn_conv)
    y_sample, (s_dn_conv, s_dn_ssm, s_swa_k, s_swa_v, s_ffn_conv) = _trunk(
        x_sample, c_sample, sample_states, layer_w, rel_bias, final_norm_w)
    return (y_prompt, y_sample, p_dn_conv, s_dn_conv, p_dn_ssm, s_dn_ssm, p_swa_k, s_swa_k,
            p_swa_v, s_swa_v, p_ffn_conv, s_ffn_conv)
```

```python
import contextlib
import concourse.bass as bass
import concourse.mybir as mybir

F32 = mybir.dt.float32
BF16 = mybir.dt.bfloat16
I32 = mybir.dt.int32
AF = mybir.ActivationFunctionType
ALU = mybir.AluOpType
AX = mybir.AxisListType

ENGS = ("pe", "act", "dve", "pool", "sp")
NSLOT = 12


class Tok:
    __slots__ = ("name", "w", "r", "excl")

    def __init__(self, name, excl=False):
        self.name = name
        self.w = None
        self.r = []
        self.excl = excl


class Op:
    __slots__ = ("eng", "idx", "fn", "deps", "dma", "slot", "slotval", "sig", "sigval")

    def __init__(self, eng, idx, fn, dma):
        self.eng = eng
        self.idx = idx
        self.fn = fn
        self.deps = []
        self.dma = dma
        self.slot = None
        self.slotval = None
        self.sig = False
        self.sigval = None


class Buf:
    def __init__(self, t, name, excl=False):
        self.t = t
        self.name = name
        self.toks = {}
        self.excl = excl

    def tok(self, key=None):
        if self.excl:
            key = None
        tk = self.toks.get(key)
        if tk is None:
            tk = Tok(f"{self.name}:{key}", self.excl)
            self.toks[key] = tk
        return tk

    def __getitem__(self, sl):
        return V(self.t[sl], [self.tok(None)])

    def v(self, sl, key):
        if not isinstance(key, (list, tuple)):
            key = [key]
        return V(self.t[sl], [self.tok(k) for k in key])


class V:
    __slots__ = ("ap", "toks")

    def __init__(self, ap, toks):
        self.ap = ap
        self.toks = toks

    def __getitem__(self, sl):
        return V(self.ap[sl], self.toks)

    def re(self, pat, **kw):
        return V(self.ap.rearrange(pat, **kw), self.toks)

    def bc(self, shape):
        return V(self.ap.broadcast_to(shape), self.toks)


def _ap(x):
    return x.ap if isinstance(x, V) else x


class Prog:
    def __init__(self, nc, same_eng_sync=True):
        self.nc = nc
        self.q = {e: [] for e in ENGS}
        self.ndma = {e: 0 for e in ENGS}
        self.slot_last = {e: [None] * NSLOT for e in ENGS}
        self.same = same_eng_sync
        self.es = contextlib.ExitStack()
        self.nbuf = 0

    def sbuf(self, shape, dtype, name=None):
        self.nbuf += 1
        name = name or f"sb{self.nbuf}"
        t = self.es.enter_context(self.nc.sbuf_tensor(name, list(shape), dtype))
        return Buf(t, name)

    def psum(self, shape, dtype, name=None):
        self.nbuf += 1
        name = name or f"ps{self.nbuf}"
        t = self.es.enter_context(self.nc.psum_tensor(name, list(shape), dtype))
        return Buf(t, name, excl=True)

    def dram(self, name, shape, dtype, kind):
        t = self.nc.dram_tensor(name, list(shape), dtype, kind=kind)
        return Buf(t.ap(), name)

    def init_arena(self, nbytes):
        self.arena = self.es.enter_context(self.nc.sbuf_tensor("arena", [128, nbytes // 4], F32))
        self.aoff = 0
        self.asize = nbytes
        self.bar = None
        self.bar_seen = set()

    def alloc(self, shape, dtype, name=None):
        self.nbuf += 1
        name = name or f"a{self.nbuf}"
        free = 1
        for d in shape[1:]:
            free *= d
        esz = 2 if dtype == BF16 else 4
        nb = (free * esz + 31) // 32 * 32
        off = self.aoff
        assert off + nb <= self.asize, f"arena overflow {name} {off}+{nb}>{self.asize}"
        self.aoff = off + nb
        ap = self.arena[0:shape[0], off // 4:(off + nb) // 4]
        if dtype != F32:
            ap = ap.bitcast(dtype)
        ap = ap[:, 0:free]
        if len(shape) == 3:
            ap = ap.rearrange("p (a b) -> p a b", b=shape[2])
        elif len(shape) == 4:
            ap = ap.rearrange("p (a b c) -> p a b c", b=shape[2], c=shape[3])
        return Buf(ap, name)

    def mark(self):
        return self.aoff

    def release(self, mark):
        self.aoff = mark
        self.barrier()

    def barrier(self):
        bar = []
        for e in ENGS:
            if self.q[e]:
                bar.append(self.q[e][-1])
            for s in self.slot_last[e]:
                if s is not None:
                    bar.append(s)
        self.bar = bar
        self.bar_seen = set()

    def op(self, eng, fn, reads=(), writes=(), dma=False):
        q = self.q[eng]
        o = Op(eng, len(q), fn, dma)
        if getattr(self, "bar", None) and eng not in self.bar_seen:
            self.bar_seen.add(eng)
            o.deps.extend(d for d in self.bar if d is not o)
        rt, wt = [], []
        for x in reads:
            if isinstance(x, V):
                for tk in x.toks:
                    (wt if tk.excl else rt).append(tk)
            elif isinstance(x, Tok):
                (wt if x.excl else rt).append(x)
        for x in writes:
            if isinstance(x, V):
                wt.extend(x.toks)
            elif isinstance(x, Tok):
                wt.append(x)
        deps = []
        for t in rt:
            if t.w is not None:
                deps.append(t.w)
        for t in wt:
            if t.w is not None:
                deps.append(t.w)
            deps.extend(t.r)
        for d in deps:
            if d is o:
                continue
            if d.eng == eng and not d.dma:
                if eng == "pe" or eng == "sp" or not self.same:
                    continue
                if d.idx < len(q) - 1:
                    continue
            o.deps.append(d)
        if dma:
            n = self.ndma[eng]
            self.ndma[eng] = n + 1
            o.slot = n % NSLOT
            o.slotval = 16 * (n // NSLOT + 1)
            prev = self.slot_last[eng][o.slot]
            if prev is not None:
                o.deps.append(prev)
            self.slot_last[eng][o.slot] = o
        for t in rt:
            t.r.append(o)
        for t in wt:
            t.w = o
            t.r = []
        q.append(o)
        return o

    def dma(self, out, in_, eng="sp", **kw):
        return self.op(eng, lambda e: e.dma_start(out=_ap(out), in_=_ap(in_), **kw),
                       reads=[in_], writes=[out], dma=True)

    def mm(self, out, lhsT, rhs, start=True, stop=True, **kw):
        return self.op("pe", lambda e: e.matmul(_ap(out), _ap(lhsT), _ap(rhs), start=start, stop=stop, **kw),
                       reads=[lhsT, rhs], writes=[out])

    def tr(self, out, in_, ident):
        if _ap(in_).dtype == F32:
            return self.mm(out, in_, ident)
        return self.op("pe", lambda e: e.transpose(_ap(out), _ap(in_), _ap(ident)),
                       reads=[in_, ident], writes=[out])

    def act(self, out, in_, func, bias=None, scale=None, accum_out=None, eng="act"):
        kw = {}
        rd = [in_]
        wr = [out]
        if bias is not None:
            kw["bias"] = _ap(bias)
            rd.append(bias)
        if scale is not None:
            kw["scale"] = _ap(scale)
            rd.append(scale)
        if accum_out is not None:
            kw["accum_out"] = _ap(accum_out)
            wr.append(accum_out)
        return self.op(eng, lambda e: e.activation(_ap(out), _ap(in_), func, **kw), reads=rd, writes=wr)

    def tt(self, out, in0, in1, op, eng="dve"):
        return self.op(eng, lambda e: e.tensor_tensor(_ap(out), _ap(in0), _ap(in1), op),
                       reads=[in0, in1], writes=[out])

    def ts(self, out, in0, s1, op0, s2=None, op1=None, accum_out=None, eng="dve"):
        rd = [in0, s1, s2]
        wr = [out, accum_out]
        kw = {}
        if op1 is not None:
            kw["op1"] = op1
        if accum_out is not None:
            kw["accum_out"] = _ap(accum_out)
        return self.op(eng, lambda e: e.tensor_scalar(_ap(out), _ap(in0), _ap(s1), _ap(s2), op0, **kw),
                       reads=rd, writes=wr)

    def stt(self, out, in0, scalar, in1, op0, op1, eng="dve"):
        return self.op(eng, lambda e: e.scalar_tensor_tensor(_ap(out), _ap(in0), _ap(scalar), _ap(in1), op0, op1),
                       reads=[in0, scalar, in1], writes=[out])

    def copy(self, out, in_, eng="dve"):
        if eng == "act":
            return self.op(eng, lambda e: e.copy(_ap(out), _ap(in_)), reads=[in_], writes=[out])
        return self.op(eng, lambda e: e.tensor_copy(_ap(out), _ap(in_)), reads=[in_], writes=[out])

    def memset(self, out, val, eng="dve"):
        return self.op(eng, lambda e: e.memset(_ap(out), val), writes=[out])

    def reduce(self, out, in_, op, axis=AX.X, eng="dve"):
        return self.op(eng, lambda e: e.tensor_reduce(_ap(out), _ap(in_), axis, op), reads=[in_], writes=[out])

    def recip(self, out, in_, eng="dve"):
        return self.op(eng, lambda e: e.reciprocal(_ap(out), _ap(in_)), reads=[in_], writes=[out])

    def generic(self, eng, fn, reads=(), writes=()):
        return self.op(eng, fn, reads=reads, writes=writes)

    def emit(self, final_waits=()):
        nc = self.nc
        for e in ENGS:
            for o in self.q[e]:
                for d in o.deps:
                    if not d.dma:
                        d.sig = True
        final = []
        for e in ENGS:
            if self.q[e]:
                last = self.q[e][-1]
                if not last.dma:
                    last.sig = True
                final.append(last)
                for s in self.slot_last[e]:
                    if s is not None:
                        final.append(s)
        for e in ENGS:
            c = 0
            for o in self.q[e]:
                if o.sig:
                    c += 1
                    o.sigval = c
        print('SIGCOUNTS', {e: sum(1 for o in self.q[e] if o.sig) for e in ENGS}, {e: len(self.q[e]) for e in ENGS}, flush=True)
        es = self.es
        csem = {e: es.enter_context(nc.semaphore(f"c_{e}")) for e in ENGS}
        dsem = {e: [es.enter_context(nc.semaphore(f"d_{e}_{i}")) for i in range(NSLOT)]
                for e in ENGS if self.ndma[e] > 0}
        block = es.enter_context(nc.Block())
        engobj = {"pe": block.tensor, "act": block.scalar, "dve": block.vector,
                  "pool": block.gpsimd, "sp": block.sync}

        def make(ename):
            def body(eng):
                seen = {}

                def wait_for(d):
                    if d.dma:
                        key = ("d", d.eng, d.slot)
                        val = d.slotval
                        sem = dsem[d.eng][d.slot]
                    else:
                        key = ("c", d.eng)
                        val = d.sigval
                        sem = csem[d.eng]
                    if seen.get(key, 0) >= val:
                        return
                    seen[key] = val
                    eng.wait_ge(sem, val)

                for o in self.q[ename]:
                    for d in o.deps:
                        wait_for(d)
                    ins = o.fn(eng)
                    if o.dma:
                        ins.then_inc(dsem[ename][o.slot], 16)
                    elif o.sig:
                        ins.then_inc(csem[ename], 1)
                    if not o.dma:
                        seen[("c", ename)] = max(seen.get(("c", ename), 0), 0)
                if ename == "sp":
                    for d in final:
                        wait_for(d)
            return body

        for e in ENGS:
            if self.q[e] or e == "sp":
                engobj[e](make(e))

    def close(self):
        self.es.close()


import math
import numpy as np
from concourse.bass_utils import run_bass_kernel_spmd

D = 1024
SEQ = 8192
NT_SEG = 18
EPS = 1e-6
NEG = -1e30
DFF = 2816
NFC = 44


def build(phases=("dn", "dns", "tok", "toks", "ffn", "ffns"), nchunks=64, ntiles=NT_SEG):
    nc = bass.Bass("TRN2", target_bir_lowering=False)
    P = Prog(nc)
    di = lambda n, s, dt=F32: P.dram(n, s, dt, "ExternalInput")
    do = lambda n, s, dt=F32: P.dram(n, s, dt, "ExternalOutput")
    xfull = di("xfull", [SEQ, D])
    xseg = di("xseg", [NT_SEG * 128, D])
    xsam = di("xsam", [128, D])
    cexp = di("cexp", [2, 128, D])
    w_ada = di("w_ada", [D, 6 * D])
    b_ada = di("b_ada", [1, 6 * D])
    w_in = di("w_in", [D, 2824])
    w_out = di("w_out", [D, D])
    w_up = di("w_up", [D, 5632])
    w_down = di("w_down", [DFF, D])
    normw = di("normw", [3, D])
    dn_convT = di("dn_convT", [128, 12, 4])
    ffn_convT = di("ffn_convT", [128, NFC, 4])
    dn_small = di("dn_small", [1, 136])
    sinks = di("sinks", [1, 8])
    rel_bias = di("rel_bias", [32, 8])
    cst = di("cst", [128, 5, 128])
    oh = di("oh", [32, 384])
    swa_mask = di("swa_mask", [128, 256])
    halo_mask = di("halo_mask", [128, 256])
    keep = di("keep", [128, 1])
    st_conv = di("st_conv", [48, 1536])
    st_ssm = di("st_ssm", [16, 4, 128, 128])
    ck = di("ck", [16, 128, 128])
    cv = di("cv", [16, 128, 128])
    st_ffn = di("st_ffn", [32, 5632])
    odn_idx = di("odn_idx", [128, 17 * 4], I32)
    w_dnp = di("w_dnp", [D, 514])
    dcw_p = di("dcw_p", [128, 3, 4])
    dsm_p = di("dsm_p", [1, 2])

    y_seg = do("y_seg", [2048, D])
    y_s = do("y_s", [128, D])
    pdc = do("pdc", [3, 3, 128])
    sdc = do("sdc", [16, 3, 1536])
    pss = do("pss", [128, 128])
    sss = do("sss", [16, 4, 128, 128])
    psk = do("psk", [128, 128])
    ssk = do("ssk", [16, 128, 128])
    psv = do("psv", [128, 128])
    ssv = do("ssv", [16, 128, 128])
    pfc = do("pfc", [2, 5632])
    sfc = do("sfc", [16, 2, 5632])

    odn = P.dram("odn", [4, 4, 2048, 128], BF16, "Internal")
    odn_mine = P.dram("odn_mine", [SEQ, 128], BF16, "Internal")
    odn_s = P.dram("odn_s", [4, 128, 128], BF16, "Internal")
    xmid = P.dram("xmid", [17 * 128, D], F32, "Internal")
    xmid_s = P.dram("xmid_s", [128, D], F32, "Internal")
    btab = P.dram("btab", [8, 384], F32, "Internal")
    mod2_d = P.dram("mod2_d", [2, 128, 3, D], F32, "Internal")
    oswa_s = P.dram("oswa_s", [16, 8, 512], BF16, "Internal")

    P.init_arena(206 * 1024)
    pb = [P.psum([128, 512], F32, f"pb{i}") for i in range(8)]

    def pq(k, q, rows=128, cols=128):
        return pb[k].v((slice(0, rows), slice(q * 128, q * 128 + cols)), q)

    def pbf(k):
        return V(pb[k].t[:, :].bitcast(BF16), [pb[k].tok(q) for q in range(4)])

    def pfull(k, lo=0, hi=512):
        return V(pb[k].t[:, lo:hi], [pb[k].tok(q) for q in range(lo // 128, (hi + 127) // 128)])

    C_ = P.alloc([128, 5, 128], F32, "cst")
    P.dma(C_[:, :, :], cst[:, :, :])
    ident = C_[:, 0, :]
    maskSL = C_[:, 1, :]
    maskUI = C_[:, 2, :]
    Jm = C_[:, 3, :]
    ones = C_[:, 4, :]
    identb_ = P.alloc([128, 128], BF16, "identb")
    P.copy(identb_[:, :], ident)
    identb = identb_[:, :]
    epsb = P.alloc([128, 1], F32, "eps")
    P.memset(epsb[:, :], EPS)
    oneb = P.alloc([128, 1], F32, "one")
    P.memset(oneb[:, :], 1.0)
    nwf = P.alloc([128, D], F32, "nwf")
    P.dma(nwf[:, :], V(normw.t[2:3, :].broadcast_to([128, D]), [normw.tok()]))
    dsm = P.alloc([128, 136], F32, "dsm")
    P.dma(dsm[:, :], V(dn_small.t[0:1, :].broadcast_to([128, 136]), [dn_small.tok()]))
    negA = P.alloc([128, 4], F32, "negA")
    P.act(negA[:, :], dsm[:, 0:4], AF.Exp)
    P.ts(negA[:, :], negA[:, :], -1.0, ALU.mult)
    dtb = dsm[:, 4:8]
    dnw = dsm[:, 8:136]
    snk = P.alloc([128, 8], F32, "snk")
    P.dma(snk[:, :], V(sinks.t[0:1, :].broadcast_to([128, 8]), [sinks.tok()]))
    dcw = P.alloc([128, 12, 4], F32, "dcw")
    P.dma(dcw[:, :, :], dn_convT[:, :, :])
    fcw = P.alloc([128, NFC, 4], F32, "fcw")
    P.dma(fcw[:, :, :], ffn_convT[:, :, :])
    keep_t = P.alloc([128, 1], F32, "keep")
    P.dma(keep_t[:, :], keep[:, :])
    mQ = P.mark()
    MOD1 = [P.alloc([128, 3, D], F32, f"mod1_{g}") for g in range(2)]
    mW = P.mark()
    MOD2 = [P.alloc([128, 3, D], F32, f"mod2_{g}") for g in range(2)]
    nw = P.alloc([128, 2, D], F32, "nw")
    for i in range(2):
        P.dma(nw[:, i, :], V(normw.t[i:i + 1, :].broadcast_to([128, D]), [normw.tok()]))

    scT = P.alloc([128, 2, 8, 128], BF16, "scT")
    for g in range(2):
        ct = P.alloc([128, D], F32, f"ct{g}")
        cb = P.alloc([128, D], BF16, f"cb{g}")
        P.dma(ct[:, :], cexp[g])
        P.act(cb[:, :], ct[:, :], AF.Silu)
        for kt in range(8):
            P.tr(pbf(3 + g)[:, kt * 128:(kt + 1) * 128], cb[:, kt * 128:(kt + 1) * 128], identb)
        P.copy(scT[:, g, :, :].re("p a b -> p (a b)"), pbf(3 + g))
    wab = [P.alloc([128, 8, 512], BF16, f"wab{i}") for i in range(2)]
    bab = [P.alloc([128, 512], F32, f"bab{i}") for i in range(2)]
    mtmp = [P.alloc([128, 512], F32, f"mtmp{i}") for i in range(2)]
    for j in range(12):
        wb = wab[j % 2]
        bb = bab[j % 2]
        P.dma(wb[:, :, :], V(w_ada.t[:, j * 512:(j + 1) * 512].rearrange("(kt p) n -> p kt n", p=128), [w_ada.tok()]),
              eng="pool")
        P.dma(bb[:, :], V(b_ada.t[0:1, j * 512:(j + 1) * 512].broadcast_to([128, 512]), [b_ada.tok()]))
        comp, half = j // 2, j % 2
        for g in range(2):
            ps = pfull(5 + g)
            for kt in range(8):
                P.mm(ps, scT[:, g, kt, :], wb[:, kt, :], start=(kt == 0), stop=(kt == 7))
            dst = (MOD1 if comp < 3 else MOD2)[g][:, comp % 3, half * 512:(half + 1) * 512]
            if comp % 3 == 1:
                t = mtmp[g]
                P.tt(t[:, :], ps, bb[:, :], ALU.add)
                P.stt(dst, t[:, :], 1.0, nw[:, 0 if comp < 3 else 1, half * 512:(half + 1) * 512], ALU.add, ALU.mult)
            else:
                P.tt(dst, ps, bb[:, :], ALU.add)
    for g in range(2):
        P.dma(mod2_d[g], MOD2[g][:, :, :])
    P.release(mW)

    def norm_mod_T(xt, A, SH, hT, wk):
        junk, ssq, hm, hb = wk["junk"], wk["ssq"], wk["hm"], wk["hb"]
        P.act(junk[:, :], xt, AF.Square, accum_out=ssq[:, :])
        P.ts(ssq[:, :], ssq[:, :], 1.0 / D, ALU.mult, EPS, ALU.add)
        P.act(ssq[:, :], ssq[:, :], AF.Ln)
        P.act(ssq[:, :], ssq[:, :], AF.Exp, scale=-0.5)
        P.stt(hm[:, :], xt, ssq[:, 0:1], A, ALU.mult, ALU.mult)
        P.tt(hb[:, :], hm[:, :], SH, ALU.add)
        bank = wk["tbank"]
        for kt in range(8):
            P.tr(pbf(bank)[:, kt * 128:(kt + 1) * 128], hb[:, kt * 128:(kt + 1) * 128], identb)
        hTv = hT[:, :, :] if isinstance(hT, Buf) else hT
        P.copy(hTv, pbf(bank).re("p (a b) -> p a b", b=128), eng="act")

    def load_w(dst, src_ap, tok):
        P.dma(dst, V(src_ap.rearrange("(kt p) n -> p kt n", p=128), [tok]), eng="pool")

    class Chain:
        def __init__(self, b0, b1):
            self.banks = (b0, b1)
            self.n = 0

        def ps(self, rows=128, cols=128):
            n = self.n
            self.n += 1
            bank = self.banks[n % 2]
            q = (n // 2) % 4
            return pb[bank].v((slice(0, rows), slice(q * 128, q * 128 + cols)), q)

    def dn_chunk(C, L, nA, W, qT, kT, vT, S_in, S_out, o_dst, ch, s_ready=None, s_done=None, zsv=None, betav=None, eav=None):
        c = slice(0, C)
        ps = ch.ps
        zsv = W["zs"][:, c] if zsv is None else zsv
        betav = W["beta"][:, c] if betav is None else betav
        eav = W["ea"][:, c] if eav is None else eav
        sq, rn = W["sq"], W["rn"]
        P.tt(sq[:, 0, c], qT, qT, ALU.mult)
        P.tt(sq[:, 1, c], kT, kT, ALU.mult)
        for i in range(2):
            sp_ = ps(128, C)
            P.mm(sp_, ones, sq[:, i, c])
            P.act(rn[:, i, c], sp_, AF.Ln, bias=epsb[:, 0:1])
        P.act(rn[:, :, c], rn[:, :, c], AF.Exp, scale=-0.5)
        yield
        gbc = W["gbc"]
        P.act(gbc[:, c], eav, AF.Ln, bias=oneb[:, 0:1])
        P.ts(gbc[:, c], gbc[:, c], nA, ALU.mult)
        gp = ps(C)
        P.mm(gp, gbc[:, c], ident)
        gtr = W["gtr"]
        P.copy(gtr[c, :], gp, eng="act")
        yield
        Gbc = ps(128, C)
        P.mm(Gbc, gtr[c, :], maskUI[c, c])
        Gtk = ps(C)
        P.mm(Gtk, maskUI[c, c], gtr[c, :])
        eG, glast, cdec = W["eG"], W["glast"], W["cdec"]
        P.act(eG[:, c], Gbc, AF.Exp)
        P.copy(glast[:, :], Gbc[:, C - 1:C], eng="act")
        P.act(cdec[:, :], glast[:, :], AF.Exp)
        tks = W["tks"]
        P.act(tks[c, 0:1], Gtk[:, 0:1], AF.Exp)
        P.act(tks[c, 2:3], Gtk[:, 0:1], AF.Exp, scale=-1.0, bias=glast[c, 0:1])
        gtk, Dm, DTm = W["gtk"], W["kp"], W["km"]
        P.copy(gtk[c, 0:1], Gtk[:, 0:1], eng="act")
        P.ts(Dm[c, c], Gbc[c, :], gtk[c, 0:1], ALU.subtract, 0.0, ALU.max)
        P.ts(DTm[c, c], Gbc[c, :], gtk[c, 0:1], ALU.subtract, 0.0, ALU.min)
        P.act(Dm[c, c], Dm[c, c], AF.Exp, scale=-1.0)
        P.act(DTm[c, c], DTm[c, c], AF.Exp)
        P.tt(Dm[c, c], Dm[c, c], maskSL[c, c], ALU.mult)
        P.stt(DTm[c, c], DTm[c, c], 128.0 ** -0.5, maskUI[c, c], ALU.mult, ALU.mult)
        yield
        def tposed(src):
            t = ps(C)
            if _ap(src).dtype == BF16:
                tb = V(_ap(t).bitcast(BF16)[:, 0:128], t.toks)
                P.tr(tb, src, identb)
                return tb
            P.mm(t, src, ident)
            return t
        bp = tposed(betav)
        P.copy(tks[c, 3:4], bp[:, 0:1], eng="act")
        P.tt(tks[c, 1:2], tks[c, 0:1], tks[c, 3:4], ALU.mult)
        qp, kn, qn = W["qp"], W["kn"], W["enG"]
        P.tt(qn[:, c], qT, rn[:, 0, c], ALU.mult)
        P.stt(qp[:, c], qn[:, c], 128.0 ** -0.5, eG[:, c], ALU.mult, ALU.mult)
        P.tt(kn[:, c], kT, rn[:, 1, c], ALU.mult)
        yield
        bkp, ktl, bv, zt = W["bkp"], W["ktl"], W["bv"], W["zt"]
        knp = ps(C)
        P.mm(knp, kn[:, c], ident)
        P.act(bkp[c, :], knp, AF.Identity, scale=tks[c, 1:2])
        P.act(ktl[c, :], knp, AF.Identity, scale=tks[c, 2:3])
        vp = tposed(vT)
        P.act(bv[c, :], vp, AF.Identity, scale=tks[c, 3:4])
        zp = tposed(zsv)
        P.tt(zt[c, :], zp, dnw[c, :], ALU.mult)
        yield
        A, AT, QKT, X = W["A"], W["AT"], W["QKT"], W["X"]
        Ap = ps(C, C)
        P.mm(Ap, kn[:, c], kn[:, c])
        P.stt(A[c, c], Ap, tks[c, 3:4], Dm[c, c], ALU.mult, ALU.mult)
        Qp = ps(C, C)
        P.mm(Qp, kn[:, c], qn[:, c])
        P.tt(QKT[c, c], Qp, DTm[c, c], ALU.mult)
        yield
        ATp = ps(C, C)
        P.mm(ATp, A[c, c], ident[c, c])
        P.copy(AT[c, c], ATp, eng="act")
        P.tt(X[c, c], ident[c, c], AT[c, c], ALU.subtract)
        yield
        Pm, PT = A, AT
        for lv in range(L):
            Pn, PTn = W["P"][lv % 2], W["PT"][lv % 2]
            p1 = ps(C, C)
            P.mm(p1, PT[c, c], Pm[c, c])
            P.copy(Pn[c, c], p1, eng="act")
            if lv < L - 1:
                p2 = ps(C, C)
                P.mm(p2, Pm[c, c], PT[c, c])
                P.copy(PTn[c, c], p2)
            yield
            p3 = ps(C, C)
            P.mm(p3, Pn[c, c], X[c, c])
            P.tt(X[c, c], X[c, c], p3, ALU.add)
            Pm, PT = Pn, PTn
            yield
        nwk, u = W["nwk"], W["u"]
        wp = ps(128, C)
        P.mm(wp, bkp[c, :], X[c, c])
        P.act(nwk[:, c], wp, AF.Identity, scale=-1.0)
        yield
        while s_ready is not None and not s_ready():
            yield
        up_ = ps(C)
        P.mm(up_, X[c, c], bv[c, :], start=True, stop=False)
        P.mm(up_, nwk[:, c], S_in, start=False, stop=True)
        P.copy(u[c, :], up_)
        yield
        op_ = ps(C)
        P.mm(op_, qp[:, c], S_in, start=True, stop=False)
        P.mm(op_, QKT[c, c], u[c, :], start=False, stop=True)
        sp2 = ps()
        P.mm(sp2, ktl[c, :], u[c, :])
        P.stt(S_out, S_in, cdec[:, 0:1], sp2, ALU.mult, ALU.add)
        if s_done is not None:
            s_done()
        oss, ojunk, of = W["oss"], W["ojunk"], W["of"]
        P.act(ojunk[c, :], op_, AF.Square, accum_out=oss[c, :])
        yield
        P.ts(oss[c, :], oss[c, :], 1.0 / 128, ALU.mult, EPS, ALU.add)
        P.act(oss[c, :], oss[c, :], AF.Ln)
        P.act(oss[c, :], oss[c, :], AF.Exp, scale=-0.5)
        P.stt(of[c, :], op_, oss[c, 0:1], zt[c, :], ALU.mult, ALU.mult)
        P.dma(o_dst, of[c, :])
        yield

    def run_window(factories, width, slotted=False):
        pending = list(factories)
        active = []
        free = list(range(width))
        while pending or active:
            while pending and free:
                sl = free.pop(0)
                f = pending.pop(0)
                active.append((sl, f(sl) if slotted else f()))
            nxt = []
            for sl, g in active:
                try:
                    next(g)
                    nxt.append((sl, g))
                except StopIteration:
                    free.append(sl)
            active = nxt

    def run_multi(queues):
        state = [{"pending": list(f), "free": list(range(n)), "active": []} for f, n in queues]
        while any(q["pending"] or q["active"] for q in state):
            for q in state:
                while q["pending"] and q["free"]:
                    sl = q["free"].pop(0)
                    q["active"].append((sl, q["pending"].pop(0)(sl)))
                nxt = []
                for sl, g in q["active"]:
                    try:
                        next(g)
                        nxt.append((sl, g))
                    except StopIteration:
                        q["free"].append(sl)
                q["active"] = nxt

    def run_interleaved(gens):
        gens = list(gens)
        while gens:
            nxt = []
            for g in gens:
                try:
                    next(g)
                    nxt.append(g)
                except StopIteration:
                    pass
            gens = nxt

    def dn_wset(i):
        W = {}
        f = lambda n, s, dt=F32: P.alloc(s, dt, f"dn{i}_{n}")
        W["sq"] = f("sq", [128, 2, 128]); W["rn"] = f("rn", [128, 2, 128])
        for n in ("gbc", "gtr", "eG", "enG", "qp", "kn", "kp", "km", "bkp", "ktl", "bv", "zt", "A", "AT", "QKT", "X",
                  "nwk", "u", "ojunk", "zs", "beta", "ea"):
            W[n] = f(n, [128, 128])
        W["P"] = [f("P0", [128, 128]), f("P1", [128, 128])]
        W["PT"] = [f("PT0", [128, 128]), f("PT1", [128, 128])]
        W["glast"] = f("glast", [128, 1]); W["cdec"] = f("cdec", [128, 1]); W["tks"] = f("tks", [128, 4])
        W["gtk"] = f("gtk", [128, 1])
        W["oss"] = f("oss", [128, 1]); W["of"] = f("of", [128, 128], BF16)
        return W

    if "dn" in phases or "dns" in phases:
        Ws = [dn_wset(i) for i in range(4)]
        _hm = P.alloc([128, D], F32, "hm")
        xw = {"junk": _hm, "ssq": P.alloc([128, 1], F32, "ssq"),
              "hm": _hm, "hb": P.alloc([128, D], BF16, "hb"), "tbank": 0}
        xts = [P.alloc([128, D], F32, "xt0")]
        hTs = [P.alloc([128, 8, 128], BF16, "hT0")]
        mS = P.mark()
        if "dn" in phases:
            xts.append(P.alloc([128, D], F32, "xt1"))
            hT4 = [P.alloc([128, 8, 512], BF16, f"hT4_{i}") for i in range(2)]
            rawg = [[P.alloc([128, 515], F32, f"rawg{i}_{g}") for g in range(3)] for i in range(2)]
            cvg = [[P.alloc([128, 512], F32, f"cvg{i}_{g}") for g in range(3)] for i in range(2)]
            zsg = [P.alloc([128, 512], BF16, f"zsg{i}") for i in range(2)]
            betag = [P.alloc([128, 512], BF16, f"betag{i}") for i in range(2)]
            betaf = P.alloc([128, 512], F32, "betaf")
            cvgv = [P.alloc([128, 512], BF16, f"cvgv{i}") for i in range(2)]
            eag = [P.alloc([128, 512], F32, f"eag{i}") for i in range(2)]
            wdp = [P.alloc([128, 8, 128], BF16, f"wdp{g}") for g in range(6)]
            wbap = P.alloc([128, 8, 2], F32, "wbap")
            P.dma(wbap[:, :, :], V(w_dnp.t[:, 512:514].rearrange("(kt p) n -> p kt n", p=128), [w_dnp.tok()]))
            for g in range(4):
                load_w(wdp[g][:, :, :], w_dnp.t[:, g * 128:(g + 1) * 128], w_dnp.tok())
            for g in range(2):
                P.copy(wdp[4 + g][:, :, :], wbap[:, :, g:g + 1].bc([128, 8, 128]))
            dcwp = P.alloc([128, 3, 4], F32, "dcwp")
            P.dma(dcwp[:, :, :], dcw_p[:, :, :])
            dsp = P.alloc([128, 2], F32, "dsp")
            P.dma(dsp[:, :], V(dsm_p.t[0:1, :].broadcast_to([128, 2]), [dsm_p.tok()]))
            negAp = P.alloc([128, 1], F32, "negAp")
            P.act(negAp[:, :], dsp[:, 0:1], AF.Exp)
            P.ts(negAp[:, :], negAp[:, :], -1.0, ALU.mult)
            Sbuf = [P.alloc([128, 128], F32, f"S_{i}") for i in range(2)]
            pdct = P.alloc([3, 3, 128], F32, "pdct")

        chains = [Chain(0, 1), Chain(2, 3), Chain(4, 5), Chain(6, 7)]

        def dn_proj(wl, hT, ncol, ch):
            outs = []
            for g in range(6):
                dst = ch.ps(128, ncol)
                for kt in range(8):
                    P.mm(dst, wl[g][:, kt, :], hT[:, kt, 0:ncol], start=(kt == 0), stop=(kt == 7))
                outs.append(dst)
            return outs

        if "dn" in phases:
            P.memset(Sbuf[0][:, :], 0.0)
            for g in range(3):
                P.memset(rawg[0][g][:, 0:3], 0.0)
            flags = {}
            ngroups = nchunks // 4
            cchains = [Chain(2, 3), Chain(4, 5), Chain(6, 7)]

            def group_gen(j, _slot):
                p = j % 2
                hT4_, rg, cg = hT4[p], rawg[p], cvg[p]
                while j >= 2 and not all(flags.get(("Cdone", ci)) for ci in range((j - 2) * 4, (j - 1) * 4)):
                    yield
                for i in range(4):
                    ci = j * 4 + i
                    xt = xts[i % 2]
                    P.dma(xt[:, :], xfull[ci * 128:(ci + 1) * 128, :])
                    norm_mod_T(xt[:, :], MOD1[0][:, 1, :], MOD1[0][:, 0, :], hT4_[:, :, i * 128:(i + 1) * 128], xw)
                    yield
                for g in range(6):
                    ps = pfull(1)
                    for kt in range(8):
                        P.mm(ps, wdp[g][:, kt, :], hT4_[:, kt, :], start=(kt == 0), stop=(kt == 7))
                    if g < 3:
                        P.copy(rg[g][:, 3:515], ps, eng="act")
                    elif g == 3:
                        P.act(zsg[p][:, :], ps, AF.Silu)
                    elif g == 4:
                        P.act(betaf[:, :], ps, AF.Exp, scale=-1.0)
                        P.act(betaf[:, :], betaf[:, :], AF.Ln, bias=oneb[:, 0:1])
                        P.act(betag[p][:, :], betaf[:, :], AF.Exp, scale=-1.0)
                    else:
                        P.act(eag[p][:, :], ps, AF.Exp, bias=dsp[:, 1:2])
                    yield
                for g in range(3):
                    r = rg[g]
                    cw = dcwp[:, g, :]
                    o = cg[g]
                    P.copy(rawg[1 - p][g][:, 0:3], r[:, 512:515], eng="pool")
                    P.ts(o[:, :], r[:, 0:512], cw[:, 0:1], ALU.mult)
                    for jj in range(1, 4):
                        P.stt(o[:, :], r[:, jj:jj + 512], cw[:, jj:jj + 1], o[:, :], ALU.mult, ALU.add)
                    P.act((cvgv[p] if g == 2 else o)[:, :], o[:, :], AF.Silu)
                    if j == ngroups - 1:
                        tp = V(pb[1].t[0:3, 0:128], [pb[1].tok()])
                        P.mm(tp, r[:, 512:515], ident)
                        P.copy(pdct[:, g, :], tp)
                    yield
                flags[("G", j)] = True

            def chunk_gen(ci, k):
                j, i = ci // 4, ci % 4
                p = j % 2
                cs = slice(i * 128, (i + 1) * 128)
                W, ch = Ws[k], cchains[k]
                last = (ci == nchunks - 1)
                while not flags.get(("G", j)):
                    yield
                S_in, S_out = Sbuf[ci % 2], Sbuf[(ci + 1) % 2]
                yield from dn_chunk(128, 6, negAp[:, 0:1], W, cvg[p][0][:, cs], cvg[p][1][:, cs], cvgv[p][:, cs],
                                    S_in[:, :], S_out[:, :], odn_mine[ci * 128:(ci + 1) * 128, :], ch,
                                    s_ready=(lambda: ci == 0 or flags.get(("S", ci - 1))),
                                    s_done=(lambda: flags.__setitem__(("S", ci), True)),
                                    zsv=zsg[p][:, cs], betav=betag[p][:, cs], eav=eag[p][:, cs])
                flags[("Cdone", ci)] = True
                if last:
                    P.dma(pss[:, :], S_out[:, :])
                    P.dma(pdc[:, :, :], pdct[:, :, :])

            run_multi([
                ([(lambda k, j=j: group_gen(j, k)) for j in range(ngroups)], 1),
                ([(lambda k, ci=ci: chunk_gen(ci, k)) for ci in range(nchunks)], 3),
            ])
            import os as _os
            for j in range(0 if _os.environ.get('DBG_NOCC') else 4):
                P.op("pool", lambda e, j=j: e.collective_compute(
                    "AllGather", ALU.bypass, replica_groups=[[0, 1, 2, 3], [4, 5, 6, 7]],
                    ins=[odn_mine.t[j * 2048:(j + 1) * 2048, :]],
                    outs=[odn.t[j].rearrange("h t c -> (h t) c")]),
                    reads=[odn_mine[:, :]], writes=[odn[0]])

        P.release(mS)
        if "dns" in phases:
            wdn = [[P.alloc([128, 8, 128], BF16, f"wdn{h}_{g}") for g in range(6)] for h in range(4)]
            wba = P.alloc([128, 8, 8], F32, "wba")
            P.dma(wba[:, :, :], V(w_in.t[:, 2048:2056].rearrange("(kt p) n -> p kt n", p=128), [w_in.tok()]))
            for h in range(4):
                for g in range(4):
                    c0 = g * 512 + h * 128
                    load_w(wdn[h][g][:, :, :], w_in.t[:, c0:c0 + 128], w_in.tok())
                for g in range(2):
                    P.copy(wdn[h][4 + g][:, :, :], wba[:, :, g * 4 + h:g * 4 + h + 1].bc([128, 8, 128]))
            rawtok = P.alloc([128, 1536], F32, "rawtok")
            xt, hT = xts[0], hTs[0]
            P.dma(xt[:, :], xsam[:, :])
            norm_mod_T(xt[:, :], MOD1[1][:, 1, :], MOD1[1][:, 0, :], hT, xw)
            stc_b = P.alloc([128, 1536], F32, "stc")
            stc = stc_b
            P.dma(stc[0:48, :], st_conv[:, :])
            raws = P.alloc([128, 12, 16, 11], F32, "raws")
            cvs = P.alloc([128, 12, 128], F32, "cvs")
            zs_s = P.alloc([128, 4, 128], BF16, "zs_s")
            be_s = P.alloc([128, 4, 128], BF16, "be_s")
            be_f = P.alloc([128, 128], F32, "be_f")
            cvsv = P.alloc([128, 4, 128], BF16, "cvsv")
            ea_s = P.alloc([128, 4, 128], F32, "ea_s")
            import os
            STEP = int(os.environ.get('DBG_STEP', '9'))
            for h in range(4 if STEP >= 2 else 0):
                pr = dn_proj(wdn[h], hT, 128, chains[h])
                for g in range(3 if STEP >= 3 else 0):
                    cc = g * 4 + h
                    tp = chains[h].ps(128, 48)
                    P.mm(tp, stc[0:48, g * 512 + h * 128:g * 512 + (h + 1) * 128], ident[0:48, 0:48])
                    P.copy(raws[:, cc, :, 0:3], tp.re("p (b j) -> p b j", j=3))
                    P.copy(raws[:, cc, :, 3:11], pr[g].re("p (b t) -> p b t", t=8), eng="act")
                    o = cvs[:, cc, :].re("p (b t) -> p b t", t=8)
                    cw = dcw[:, cc, :]
                    P.ts(o, raws[:, cc, :, 0:8], cw[:, 0:1], ALU.mult)
                    for j in range(1, 4):
                        P.stt(o, raws[:, cc, :, j:j + 8], cw[:, j:j + 1], o, ALU.mult, ALU.add)
                    P.act(cvsv[:, h, :] if g == 2 else cvs[:, cc, :], cvs[:, cc, :], AF.Silu)
                P.act(zs_s[:, h, :], pr[3], AF.Silu)
                P.act(be_f[:, :], pr[4], AF.Exp, scale=-1.0)
                P.act(be_f[:, :], be_f[:, :], AF.Ln, bias=oneb[:, 0:1])
                P.act(be_s[:, h, :], be_f[:, :], AF.Exp, scale=-1.0)
                P.act(ea_s[:, h, :], pr[5], AF.Exp, bias=dtb[:, h:h + 1])
            for cc in range(12 if STEP >= 4 else 0):
                rn_ = stc_b[:, cc * 128:(cc + 1) * 128]
                P.copy(rn_.re("p (b t) -> p b t", t=8), raws[:, cc, :, 3:11], eng="pool")
                tp = chains[cc % 4].ps()
                P.mm(tp, rn_, ident)
                g, h = cc // 4, cc % 4
                P.copy(rawtok[:, g * 512 + h * 128:g * 512 + (h + 1) * 128], tp)
            for b in range(16 if STEP >= 5 else 0):
                P.dma(sdc[b], rawtok[b * 8 + 5:b * 8 + 8, :])
            Ss = [P.alloc([128, 128], F32, f"Ss{i}") for i in range(4)]
            So = [P.alloc([128, 128], F32, f"So{i}") for i in range(4)]

            def sgen(b, h, k):
                W = Ws[k]
                S_in, S_out = Ss[k], So[k]
                P.dma(S_in[:, :], st_ssm[b, h])
                cs = slice(b * 8, b * 8 + 8)
                P.copy(W["ea"][:, 0:8], ea_s[:, h, cs], eng="pool")
                yield
                yield from dn_chunk(8, 2, negA[:, h:h + 1], W, cvs[:, h, cs], cvs[:, 4 + h, cs], cvsv[:, h, cs], S_in[:, :],
                                    S_out[:, :], odn_s[h, b * 8:b * 8 + 8, :], chains[k],
                                    zsv=zs_s[:, h, cs], betav=be_s[:, h, cs])
                P.dma(sss[b, h], S_out[:, :])

            run_window([(lambda k, b=b, h=h: sgen(b, h, k)) for b in range(16) for h in range(4)], 4, slotted=True)
        P.release(mW)

    if "tok" in phases or "toks" in phases:
        biasM = P.alloc([128, 8, 256], F32, "biasM")
        biasM0 = P.alloc([128, 8, 256], F32, "biasM0")
        mB = P.mark()
        rb = P.alloc([32, 8], F32, "rb")
        oh_t = P.alloc([32, 384], F32, "oh")
        P.dma(rb[:, :], rel_bias[:, :])
        P.dma(oh_t[:, :], oh[:, :])
        P.mm(pfull(0, 0, 384)[0:8, :], rb[:, :], oh_t[:, :])
        gt = P.alloc([8, 384], F32, "gt")
        P.copy(gt[:, :], pfull(0, 0, 384)[0:8, :])
        P.dma(btab[:, :], gt[:, :])
        h2 = P.alloc([128, 8, 256], F32, "h2")
        P.dma(h2[:, :, :], V(bass.AP(btab.t.tensor, 0, [[1, 128], [384, 8], [1, 256]]), [btab.tok()]))
        smask = P.alloc([128, 256], F32, "smask")
        P.dma(smask[:, :], swa_mask[:, :])
        hmask = P.alloc([128, 256], F32, "hmask")
        P.dma(hmask[:, :], halo_mask[:, :])
        for h in range(8):
            bk = 1 + (h % 2)
            P.mm(pfull(bk, 0, 256), Jm, h2[:, h, :])
            P.tt(biasM[:, h, :], pfull(bk, 0, 256), smask[:, :], ALU.add)
            P.tt(biasM0[:, h, :], biasM[:, h, :], hmask[:, :], ALU.add)

        P.release(mB)
        wq = P.alloc([128, 8, 512], BF16, "wq")
        wkv = P.alloc([128, 8, 256], BF16, "wkv")
        wo = P.alloc([128, 8, D], BF16, "wo")
        for c in range(4):
            for j, hh in enumerate((c, 4 + c)):
                c0 = 2056 + hh * 64
                load_w(wq[:, :, c * 128 + j * 64:c * 128 + (j + 1) * 64], w_in.t[:, c0:c0 + 64], w_in.tok())
        load_w(wkv[:, :, :], w_in.t[:, 2568:2824], w_in.tok())
        load_w(wo[:, :, :], w_out.t[:, :], w_out.tok())
        _hm = P.alloc([128, D], F32, "hm")
        xw = {"junk": _hm, "ssq": P.alloc([128, 1], F32, "ssq"),
              "hm": _hm, "hb": P.alloc([128, D], BF16, "hb"), "tbank": 0}
        xts = [P.alloc([128, D], F32, f"xt{i}") for i in range(3)]
        hTs = [P.alloc([128, 8, 128], BF16, f"hT{i}") for i in range(2)]
        qTt = P.alloc([128, 4, 128], BF16, "qTt")
        kTb = P.alloc([128, 2, 128], BF16, "kTb")
        vb = P.alloc([128, 2, 128], BF16, "vb")
        kvf = P.alloc([128, 256], F32, "kvf")
        sb = [P.alloc([128, 256], F32, f"sb{i}") for i in range(4)]
        pbuf = [P.alloc([128, 256], BF16, f"pb{i}") for i in range(4)]
        pT = [P.alloc([128, 2, 128], BF16, f"pT{i}") for i in range(4)]
        st = P.alloc([128, 8, 4], F32, "st")
        mixes = [P.alloc([128, D], BF16, f"mix{i}") for i in range(2)]
        mix = mixes[0]
        mixT = P.alloc([128, 8, 128], BF16, "mixT")
        xo = [P.alloc([128, D], F32, f"xo{i}") for i in range(2)]

        def proj_swa(hT, need_q=True, qdst=None):
            qdst = qTt if qdst is None else qdst
            if need_q:
                for c in range(4):
                    for kt in range(8):
                        P.mm(pq(0, c), wq[:, kt, c * 128:(c + 1) * 128], hT[:, kt, :], start=(kt == 0), stop=(kt == 7))
                P.act(qdst[:, :, :].re("p a b -> p (a b)"), pfull(0), AF.Identity, scale=0.125)
            for kt in range(8):
                P.mm(pq(2, 0), wkv[:, kt, 0:128], hT[:, kt, :], start=(kt == 0), stop=(kt == 7))
            for kt in range(8):
                P.mm(pfull(2, 256, 512), hT[:, kt, :], wkv[:, kt, :], start=(kt == 0), stop=(kt == 7))

        oix = P.alloc([128, 17 * 4], I32, "oix")
        P.dma(oix[:, :], odn_idx[:, :])
        odn_flat = odn.t.rearrange("j h t c -> (j h t) c")

        def load_odn_gather(j, mx):
            for h in range(4):
                col = j * 4 + h
                P.op("pool", lambda e, h=h, col=col: e.indirect_dma_start(
                    out=mx.t[:, h * 128:(h + 1) * 128], out_offset=None, in_=odn_flat,
                    in_offset=bass.IndirectOffsetOnAxis(ap=oix.t[:, col:col + 1], axis=0)),
                    reads=[odn[0], oix[:, :]], writes=[mx[:, :]], dma=True)

        def out_proj_gen(xt, g, dst_dram, load_odn, mx):
            load_odn()
            yield
            for kt in range(8):
                P.tr(pbf(1)[:, kt * 128:(kt + 1) * 128], mx[:, kt * 128:(kt + 1) * 128], identb)
            yield
            P.copy(mixT[:, :, :].re("p a b -> p (a b)"), pbf(1), eng="act")
            yield
            xoo = xo[0]
            for half in range(2):
                ps = pfull(1)
                for kt in range(8):
                    P.mm(ps, mixT[:, kt, :], wo[:, kt, half * 512:(half + 1) * 512], start=(kt == 0), stop=(kt == 7))
                yield
                hs = slice(half * 512, (half + 1) * 512)
                P.tt(xoo[:, hs], ps, MOD1[g][:, 2, hs], ALU.mult)
                P.tt(xoo[:, hs], xoo[:, hs], xt[:, hs], ALU.add, eng="pool")
                yield
            P.dma(dst_dram, xoo[:, :])

        def out_proj(xt, g, dst_dram, load_odn, mx=None):
            for _ in out_proj_gen(xt, g, dst_dram, load_odn, mix if mx is None else mx):
                pass

        if "tok" in phases:
            qT2 = [qTt, P.alloc([128, 4, 128], BF16, "qTt1")]
            kT3 = P.alloc([128, 3, 128], BF16, "kT3")
            v3 = P.alloc([128, 3, 128], BF16, "v3")

            def stageA_gen(ti):
                xt, hT = xts[ti % 3], hTs[ti % 2]
                P.dma(xt[:, :], xseg[ti * 128:(ti + 1) * 128, :])
                yield
                norm_mod_T(xt[:, :], MOD1[0][:, 1, :], MOD1[0][:, 0, :], hT, xw)
                yield
                qdst = qT2[ti % 2]
                if ti > 0:
                    for c in range(4):
                        for kt in range(8):
                            P.mm(pq(0, c), wq[:, kt, c * 128:(c + 1) * 128], hT[:, kt, :], start=(kt == 0), stop=(kt == 7))
                        yield
                    P.act(qdst[:, :, :].re("p a b -> p (a b)"), pfull(0), AF.Identity, scale=0.125)
                for kt in range(8):
                    P.mm(pq(2, 0), wkv[:, kt, 0:128], hT[:, kt, :], start=(kt == 0), stop=(kt == 7))
                yield
                for kt in range(8):
                    P.mm(pfull(2, 256, 512), hT[:, kt, :], wkv[:, kt, :], start=(kt == 0), stop=(kt == 7))
                yield
                P.copy(kT3[:, ti % 3, :], pq(2, 0), eng="act")
                P.copy(kvf[:, :], pfull(2, 256, 512))
                P.copy(v3[:, ti % 3, :], kvf[:, 128:256], eng="pool")
                if ti == ntiles - 1:
                    P.dma(psk[:, :], kvf[:, 0:128])
                    P.dma(psv[:, :], kvf[:, 128:256])

            def stageA(ti):
                for _ in stageA_gen(ti):
                    pass

            def head_gen(ti, hh, k):
                kh, cq = hh // 4, hh % 4
                r0 = kh * 64
                qT_ = qT2[ti % 2]
                ks = ((ti - 1) % 3, ti % 3)
                bm = biasM0 if ti == 2 else biasM
                bs = bt = 3 + k
                for kb in range(2):
                    P.mm(V(pb[bs].t[:, kb * 128:(kb + 1) * 128], [pb[bs].tok()]), qT_[r0:r0 + 64, cq, :],
                         kT3[r0:r0 + 64, ks[kb], :])
                s_ = sb[k]
                P.tt(s_[:, :], pfull(bs, 0, 256), bm[:, hh, :], ALU.add)
                yield
                P.reduce(st[:, hh, 0:1], s_[:, :], ALU.max)
                P.tt(st[:, hh, 0:1], st[:, hh, 0:1], snk[:, hh:hh + 1], ALU.max)
                P.ts(st[:, hh, 1:2], st[:, hh, 0:1], -1.0, ALU.mult)
                yield
                pp = pbuf[k]
                P.act(pp[:, :], s_[:, :], AF.Exp, bias=st[:, hh, 1:2], accum_out=st[:, hh, 2:3])
                P.act(st[:, hh, 3:4], snk[:, hh:hh + 1], AF.Exp, bias=st[:, hh, 1:2])
                yield
                for kb in range(2):
                    P.tr(pbf(bt)[:, kb * 128:(kb + 1) * 128], pp[:, kb * 128:(kb + 1) * 128], identb)
                P.tt(st[:, hh, 3:4], st[:, hh, 3:4], st[:, hh, 2:3], ALU.add)
                yield
                ptt = pT[k]
                P.copy(ptt[:, :, :].re("p a b -> p (a b)"), pbf(bt)[:, 0:256], eng="act")
                yield
                ov = pb[7].v((slice(0, 128), slice(hh * 64, hh * 64 + 64)), 0)
                for kb in range(2):
                    P.mm(ov, ptt[:, kb, :], v3[:, ks[kb], kh * 64:(kh + 1) * 64], start=(kb == 0), stop=(kb == 1))
                yield

            def stageC(ti):
                j = ti - 1
                mx = mixes[ti % 2]
                return out_proj_gen(xts[ti % 3], 0, xmid[j * 128:(j + 1) * 128, :],
                                    lambda: load_odn_gather(j, mx), mx)

            def stageB(ti):
                qs = [([(lambda k, hh=hh: head_gen(ti, hh, k)) for hh in range(8)], 4)]
                if ti >= 2:
                    qs.append(([lambda k: stageC(ti - 1)], 1))
                if ti + 1 < ntiles:
                    qs.append(([lambda k: stageA_gen(ti + 1)], 1))
                run_multi(qs)
                mx = mixes[ti % 2]
                P.recip(st[:, :, 3], st[:, :, 3])
                P.tt(mx[:, 512:1024].re("p (h c) -> p h c", c=64), pfull(7).re("p (h c) -> p h c", c=64),
                     st[:, :, 3:4].bc([128, 8, 64]), ALU.mult)

            stageA(0)
            if ntiles > 1:
                stageA(1)
            for ti in range(1, ntiles):
                stageB(ti)
            for _ in stageC(ntiles - 1):
                pass
        if "toks" in phases:
            xt, hT = xts[0], hTs[0]
            P.dma(xt[:, :], xsam[:, :])
            norm_mod_T(xt[:, :], MOD1[1][:, 1, :], MOD1[1][:, 0, :], hT, xw)
            proj_swa(hT, need_q=True)
            kTn = P.alloc([128, 128], BF16, "kTn")
            P.copy(kTn[:, :], pq(2, 0), eng="act")
            P.copy(kvf[:, :], pfull(2, 256, 512))
            for b in range(16):
                P.dma(ssk[b, 0:120, :], ck[b, 8:128, :])
                P.dma(ssv[b, 0:120, :], cv[b, 8:128, :])
                P.dma(ssk[b, 120:128, :], kvf[b * 8:(b + 1) * 8, 0:128])
                P.dma(ssv[b, 120:128, :], kvf[b * 8:(b + 1) * 8, 128:256])
            ckb = P.alloc([128, 16, 128], BF16, "ckb")
            cvb = P.alloc([128, 16, 128], BF16, "cvb")
            P.dma(ckb[:, :, :], V(ck.t.rearrange("b k c -> k b c"), [ck.tok()]), eng="pool")
            P.dma(cvb[:, :, :], V(cv.t.rearrange("b k c -> k b c"), [cv.tok()]), eng="pool")
            vnew = P.alloc([8, 16, 128], BF16, "vnew")
            P.dma(vnew[:, :, :], V(ssv.t[:, 120:128, :].rearrange("b t c -> t b c"), [ssv.tok()]), eng="pool")
            kbT = P.alloc([128, 16, 128], BF16, "kbT")
            for half in range(2):
                for bb in range(8):
                    b = half * 8 + bb
                    P.tr(pbf(3)[:, bb * 128:(bb + 1) * 128], ckb[:, b, :], identb)
                P.copy(kbT[:, half * 8:(half + 1) * 8, :].re("p a b -> p (a b)"), pbf(3), eng="act")
            NSL = 2
            ssl = [P.alloc([8, 8, 136], F32, f"ss_{i}") for i in range(NSL)]
            spl = [P.alloc([8, 8, 136], BF16, f"sp_{i}") for i in range(NSL)]
            sttl = [P.alloc([8, 8, 4], F32, f"stt_{i}") for i in range(NSL)]
            pTcl = [P.alloc([128, 8, 8], BF16, f"pTc{i}") for i in range(NSL)]
            pTnl = [P.alloc([8, 8, 8], BF16, f"pTn{i}") for i in range(NSL)]
            o_all = P.alloc([8, 16, 512], BF16, "o_all")

            def seq_gen(b, k):
                ss_, sp_, stt_, pTc, pTn = ssl[k], spl[k], sttl[k], pTcl[k], pTnl[k]
                X, Y, Z = 2 + 3 * k, 3 + 3 * k, 4 + 3 * k
                cs = slice(b * 8, b * 8 + 8)
                for hh in range(8):
                    kh, cq = hh // 4, hh % 4
                    r0 = kh * 64
                    bank = (X, Y)[hh // 4]
                    dst = V(pb[bank].t[0:8, (hh % 4) * 128:(hh % 4 + 1) * 128], [pb[bank].tok()])
                    P.mm(dst, qTt[r0:r0 + 64, cq, cs], kbT[r0:r0 + 64, b, :])
                yield
                for hh in range(8):
                    kh, cq = hh // 4, hh % 4
                    r0 = kh * 64
                    dst = V(pb[Z].t[0:8, hh * 8:hh * 8 + 8], [pb[Z].tok()])
                    P.mm(dst, qTt[r0:r0 + 64, cq, cs], kTn[r0:r0 + 64, cs])
                yield
                for half in range(2):
                    bank = (X, Y)[half]
                    P.tt(ss_[:, half * 4:(half + 1) * 4, 0:128],
                         V(pb[bank].t[0:8, :].rearrange("p (h c) -> p h c", c=128), [pb[bank].tok()]),
                         biasM[0:8, half * 4:(half + 1) * 4, 0:128], ALU.add)
                P.tt(ss_[:, :, 128:136], V(pb[Z].t[0:8, 0:64].rearrange("p (h c) -> p h c", c=8), [pb[Z].tok()]),
                     biasM[0:8, :, 128:136], ALU.add)
                yield
                P.reduce(stt_[:, :, 0], ss_[:, :, :], ALU.max)
                P.tt(stt_[:, :, 0], stt_[:, :, 0], snk[0:8, :], ALU.max)
                P.tt(ss_[:, :, :], ss_[:, :, :], stt_[:, :, 0:1].bc([8, 8, 136]), ALU.subtract)
                yield
                P.act(sp_[:, :, :], ss_[:, :, :], AF.Exp)
                P.reduce(stt_[:, :, 2], sp_[:, :, :], ALU.add)
                P.tt(stt_[:, :, 1], snk[0:8, :], stt_[:, :, 0], ALU.subtract)
                P.act(stt_[:, :, 1], stt_[:, :, 1], AF.Exp)
                yield
                P.tt(stt_[:, :, 3], stt_[:, :, 1], stt_[:, :, 2], ALU.add)
                P.recip(stt_[:, :, 3], stt_[:, :, 3])
                zb = pbf(Z).ap
                for hh in range(8):
                    P.tr(V(zb[:, 256 + hh * 8:256 + hh * 8 + 8], [pb[Z].tok()]), sp_[:, hh, 0:128], identb[0:8, 0:8])
                    P.tr(V(zb[0:8, 512 + hh * 8:512 + hh * 8 + 8], [pb[Z].tok()]), sp_[:, hh, 128:136], identb[0:8, 0:8])
                yield
                P.copy(pTc[:, :, :].re("p a b -> p (a b)"), V(zb[:, 256:320], [pb[Z].tok()]), eng="act")
                P.copy(pTn[:, :, :].re("p a b -> p (a b)"), V(zb[0:8, 512:576], [pb[Z].tok()]), eng="act")
                yield
                for hh in range(8):
                    kh = hh // 4
                    ov = V(pb[X].t[0:8, hh * 64:hh * 64 + 64], [pb[X].tok()])
                    P.mm(ov, pTc[:, hh, :], cvb[:, b, kh * 64:(kh + 1) * 64], start=True, stop=False)
                    P.mm(ov, pTn[:, hh, :], vnew[:, b, kh * 64:(kh + 1) * 64], start=False, stop=True)
                yield
                P.tt(o_all[:, b, :].re("p (h c) -> p h c", c=64),
                     V(pb[X].t[0:8, :].rearrange("p (h c) -> p h c", c=64), [pb[X].tok()]),
                     stt_[:, :, 3:4].bc([8, 8, 64]), ALU.mult)
                yield

            run_window([(lambda k, b=b: seq_gen(b, k)) for b in range(16)], NSL, slotted=True)
            P.dma(V(oswa_s.t.rearrange("b t c -> t b c"), [oswa_s.tok()]), o_all[:, :, :])

            def load_odn_s():
                P.dma(mix[:, 0:512].re("p (h c) -> p h c", c=128), V(odn_s.t.rearrange("h t c -> t h c"), [odn_s.tok()]))
                P.dma(mix[:, 512:1024], V(oswa_s.t.rearrange("b t c -> (b t) c"), [oswa_s.tok()]))
            out_proj(xt, 1, xmid_s[:, :], load_odn_s)
        P.release(mW)
    P.release(mQ)

    if "ffn" in phases or "ffns" in phases:
        MOD2b = P.alloc([128, 3, D], F32, "mod2r")
        wu = P.alloc([128, 8, 5632], BF16, "wu")
        wd = P.alloc([128, 22, D], BF16, "wd")
        for j in (0, 5, 1, 6, 2, 7, 3, 8, 4, 9, 10):
            dst = wu.v((slice(None), slice(None), slice(j * 512, (j + 1) * 512)), ("wu", j))
            P.dma(dst, V(w_up.t[:, j * 512:(j + 1) * 512].rearrange("(kt p) n -> p kt n", p=128), [w_up.tok()]), eng="pool")
        load_w(wd[:, :, :], w_down.t[:, :], w_down.tok())
        _hm = P.alloc([128, D], F32, "hm")
        xw = {"junk": _hm, "ssq": P.alloc([128, 1], F32, "ssq"), "hm": _hm,
              "hb": P.alloc([128, D], BF16, "hb"), "tbank": 0}
        xm = P.alloc([128, D], F32, "xm")
        h2Ts = [P.alloc([128, 8, 256], BF16, f"h2T{i}") for i in range(2)]
        ug = [[P.alloc([128, 260], F32, f"ug{i}_{k}") for k in range(2)] for i in range(2)]
        cc_ = [[P.alloc([128, 256], F32, f"cc{i}_{k}") for k in range(2)] for i in range(2)]
        actT = P.alloc([128, 22, 256], BF16, "actT")
        utok = Buf(actT.t.rearrange("p a b -> p (a b)")[0:32, 0:2816].bitcast(F32), "utok")
        utok.toks = actT.toks
        uh = P.alloc([128, NFC, 2], F32, "uh")
        yo = P.alloc([128, D], F32, "yo")
        ss2 = P.alloc([128, 1], F32, "ss2")

        def ffn_prep(srcs, h2T):
            for i in range(len(srcs)):
                P.dma(xm[:, :], srcs[i])
                norm_mod_T(xm[:, :], MOD2b[:, 1, :], MOD2b[:, 0, :], h2T[:, :, i * 128:(i + 1) * 128], xw)
                yield

        def ffn_compute(srcs, g, mode, dsts, h2T):
            nt = len(srcs)
            NB = nt * 128
            for fc in range(22):
                for k in range(2):
                    fcc = fc + 22 * k
                    ps = pfull(1 + (fc % 2) * 2 + k, 0, NB)
                    for kt in range(8):
                        P.mm(ps, wu.v((slice(None), kt, slice(fcc * 128, (fcc + 1) * 128)), ("wu", fcc // 4)),
                             h2T[:, kt, 0:NB], start=(kt == 0), stop=(kt == 7))
                    cw = fcw[:, fcc, :]
                    if mode == "halo":
                        P.ts(uh[:, fcc, :], ps[:, NB - 2:NB], keep_t[:, 0:1], ALU.mult)
                        continue
                    u_ = ug[fc % 2][k]
                    o_ = cc_[fc % 2][k]
                    if mode == "prompt":
                        P.copy(u_[:, 0:2], uh[:, fcc, :], eng="act")
                        P.copy(u_[:, 2:2 + NB], ps, eng="act")
                        P.copy(uh[:, fcc, :], u_[:, NB:NB + 2], eng="act")
                        taps = [u_[:, j:j + NB] for j in range(3)]
                        ov = o_[:, 0:NB]
                    else:
                        u3 = u_[:, 0:160].re("p (b j) -> p b j", j=10)
                        P.copy(u3[:, :, 0:2], uh_s[:, fcc, :].re("p (b j) -> p b j", j=2), eng="pool")
                        P.copy(u3[:, :, 2:10], ps.re("p (b t) -> p b t", t=8), eng="act")
                        P.copy(uh_s[:, fcc, :].re("p (b j) -> p b j", j=2), u3[:, :, 8:10], eng="pool")
                        taps = [u3[:, :, j:j + 8] for j in range(3)]
                        ov = o_[:, 0:128].re("p (b t) -> p b t", t=8)
                    P.ts(ov, taps[0], cw[:, 0:1], ALU.mult, cw[:, 3:4], ALU.add)
                    P.stt(ov, taps[1], cw[:, 1:2], ov, ALU.mult, ALU.add)
                    P.stt(ov, taps[2], cw[:, 2:3], ov, ALU.mult, ALU.add)
                if mode == "halo":
                    yield
                    continue
                gt_, up_ = cc_[fc % 2]
                P.act(gt_[:, 0:NB], gt_[:, 0:NB], AF.Silu)
                P.tt(actT[:, fc, 0:NB], gt_[:, 0:NB], up_[:, 0:NB], ALU.mult, eng="pool")
                yield
            if mode == "halo":
                return
            for i in range(nt):
                P.dma(xm[:, :], srcs[i])
                for half in range(2):
                    ps = pfull(5 + half)
                    for fc in range(22):
                        P.mm(ps, actT[:, fc, i * 128:(i + 1) * 128], wd[:, fc, half * 512:(half + 1) * 512],
                             start=(fc == 0), stop=(fc == 21))
                    hs = slice(half * 512, (half + 1) * 512)
                    P.tt(yo[:, hs], ps, MOD2b[:, 2, hs], ALU.mult)
                    P.tt(yo[:, hs], yo[:, hs], xm[:, hs], ALU.add, eng="pool")
                P.act(_hm[:, :], yo[:, :], AF.Square, accum_out=ss2[:, :])
                P.ts(ss2[:, :], ss2[:, :], 1.0 / D, ALU.mult, EPS, ALU.add)
                P.act(ss2[:, :], ss2[:, :], AF.Ln)
                P.act(ss2[:, :], ss2[:, :], AF.Exp, scale=-0.5)
                P.stt(yo[:, :], yo[:, :], ss2[:, 0:1], nwf[:, :], ALU.mult, ALU.mult)
                P.dma(dsts[i], yo[:, :])
                yield

        def ffn_block(srcs, g, mode, dsts):
            for _ in ffn_prep(srcs, h2Ts[0]):
                pass
            for _ in ffn_compute(srcs, g, mode, dsts, h2Ts[0]):
                pass

        def u_tokmajor(src3, nrow, dst_dram):
            for grp in range(4):
                for i in range(11):
                    fcc = grp * 11 + i
                    bank = 1 + i % 2
                    dst = V(pb[bank].t[0:nrow, 0:128], [pb[bank].tok()])
                    P.mm(dst, src3(fcc), ident)
                    P.copy(utok[0:nrow, i * 128:(i + 1) * 128], dst)
                P.dma(dst_dram[:, grp * 1408:(grp + 1) * 1408], utok[0:nrow, :])

        if "ffn" in phases:
            P.dma(MOD2b[:, :, :], mod2_d[0])
            blocks = [([xmid[0:128, :]], "halo", None)]
            for j in range(1, ntiles - 1, 2):
                js = [jj for jj in (j, j + 1) if jj < ntiles - 1]
                blocks.append(([xmid[jj * 128:(jj + 1) * 128, :] for jj in js], "prompt",
                               [y_seg[(jj - 1) * 128:jj * 128, :] for jj in js]))
            for _ in ffn_prep(blocks[0][0], h2Ts[0]):
                pass
            for bi, (srcs_, mode_, dsts_) in enumerate(blocks):
                qs = [([lambda k, a=(srcs_, 0, mode_, dsts_, h2Ts[bi % 2]): ffn_compute(*a)], 1)]
                if bi + 1 < len(blocks):
                    qs.append(([lambda k, a=(blocks[bi + 1][0], h2Ts[(bi + 1) % 2]): ffn_prep(*a)], 1))
                run_multi(qs)
            u_tokmajor(lambda fcc: uh[:, fcc, :], 2, pfc)
        if "ffns" in phases:
            uh_s = P.alloc([128, NFC, 32], F32, "uh_s")
            for grp in range(4):
                P.dma(utok[0:32, :], st_ffn[:, grp * 1408:(grp + 1) * 1408])
                for i in range(11):
                    fcc = grp * 11 + i
                    bank = 1 + fcc % 2
                    dst = V(pb[bank].t[:, 0:32], [pb[bank].tok()])
                    P.mm(dst, utok[0:32, i * 128:(i + 1) * 128], ident[0:32, 0:32])
                    P.copy(uh_s[:, fcc, :], dst)
            P.dma(MOD2b[:, :, :], mod2_d[1])
            ffn_block([xmid_s[:, :]], 1, "sample", [y_s[:, :]])
            sfc2 = Buf(sfc.t.rearrange("b j c -> (b j) c"), "sfc2")
            u_tokmajor(lambda fcc: uh_s[:, fcc, :], 32, sfc2)
    P.emit()
    P.close()
    return nc


def _bucket(d):
    d = np.maximum(d, 0)
    logv = np.log(np.maximum(d, 1).astype(np.float32) / np.float32(16)) / np.float32(math.log(8.0))
    large = np.minimum(16 + (logv * 16).astype(np.int32), 31)
    return np.where(d < 16, d, large)


def _consts():
    i = np.arange(128)
    ident = np.eye(128, dtype=np.float32)
    maskSL = (i[:, None] > i[None, :]).astype(np.float32)
    maskUI = (i[:, None] <= i[None, :]).astype(np.float32)
    J = ident[::-1].copy()
    ones = np.ones((128, 128), np.float32)
    cst = np.stack([ident, maskSL, maskUI, J, ones], axis=1)
    e = np.arange(384)
    bk = _bucket(255 - e)
    oh = (bk[None, :] == np.arange(32)[:, None]).astype(np.float32)
    c = np.arange(256)
    dist = i[:, None] + 128 - c[None, :]
    swa_mask = np.where((dist >= 0) & (dist < 128), 0.0, NEG).astype(np.float32)
    return cst, oh, swa_mask


def _odn_idx(s):
    t = np.clip(s * 2048 - 128 + np.arange(17)[None, :, None] * 128 + np.arange(128)[:, None, None], 0, SEQ - 1)
    h = np.arange(4)[None, None, :]
    return ((t // 2048) * 8192 + h * 2048 + t % 2048).astype(np.int32).reshape(128, 68)


_NC_CACHE = {}


def kernel(x_prompt, x_sample, state_dn_conv, state_dn_ssm, cache_swa_k, cache_swa_v, state_ffn_conv,
           c_prompt, c_sample, rel_bias, final_norm_w, w_ada, b_ada, norm_mix_w, w_in, dn_conv_w,
           dn_A_log, dn_dt_bias, dn_norm_w, swa_sinks, w_out, norm_ffn_w, ffn_w_up, ffn_conv_w,
           ffn_conv_b, ffn_w_down, _phases=None, _nchunks=64, _ntiles=NT_SEG):
    f32 = lambda a: np.ascontiguousarray(np.asarray(a, dtype=np.float32))
    phases = _phases if _phases is not None else ("dn", "dns", "tok", "toks", "ffn", "ffns")
    nc = build(phases, _nchunks, _ntiles)
    cst, oh, swa_mask = _consts()
    xp = f32(x_prompt)
    xs = f32(x_sample)
    normw = np.stack([f32(norm_mix_w)[0], f32(norm_ffn_w)[0], f32(final_norm_w)])
    dn_convT = f32(dn_conv_w)[0].reshape(4, 12, 128).transpose(2, 1, 0).copy()
    fcT = np.concatenate([f32(ffn_conv_w)[0], f32(ffn_conv_b)], axis=0).reshape(4, NFC, 128).transpose(2, 1, 0).copy()
    dn_small = np.concatenate([f32(dn_A_log)[0], f32(dn_dt_bias)[0], f32(dn_norm_w)[0]])[None, :]
    shared = {
        "w_ada": f32(w_ada)[0], "b_ada": f32(b_ada), "w_in": f32(w_in)[0], "w_out": f32(w_out)[0],
        "w_up": f32(ffn_w_up)[0], "w_down": f32(ffn_w_down)[0], "normw": normw, "dn_convT": dn_convT,
        "ffn_convT": fcT, "dn_small": dn_small, "sinks": f32(swa_sinks), "rel_bias": f32(rel_bias),
        "cst": cst, "oh": oh, "swa_mask": swa_mask,
    }
    in_maps = []
    for core in range(8):
        b, s = core // 4, core % 4
        seg = np.zeros((NT_SEG * 128, D), np.float32)
        lo = s * 2048 - 256
        if lo < 0:
            seg[256:] = xp[b, 0:2048]
        else:
            seg[:] = xp[b, lo:lo + NT_SEG * 128]
        sb = slice(core * 16, core * 16 + 16)
        hm = np.zeros((128, 256), np.float32)
        if s == 0:
            hm[:, :128] = NEG
        m = dict(shared)
        m.update({
            "xfull": xp[b], "xseg": seg, "xsam": xs[sb].reshape(128, D),
            "cexp": np.stack([np.repeat(f32(c_prompt)[b:b + 1], 128, axis=0),
                              np.repeat(f32(c_sample)[sb], 8, axis=0)]),
            "halo_mask": hm, "keep": np.full((128, 1), 0.0 if s == 0 else 1.0, np.float32),
            "st_conv": f32(state_dn_conv)[0, sb].reshape(48, 1536),
            "st_ssm": f32(state_dn_ssm)[0, sb],
            "ck": f32(cache_swa_k)[0, sb].reshape(16, 128, 128),
            "cv": f32(cache_swa_v)[0, sb].reshape(16, 128, 128),
            "st_ffn": f32(state_ffn_conv)[0, sb].reshape(32, 5632),
            "w_dnp": np.ascontiguousarray(np.concatenate(
                [f32(w_in)[0][:, g * 512 + s * 128:g * 512 + (s + 1) * 128] for g in range(4)]
                + [f32(w_in)[0][:, 2048 + s:2049 + s], f32(w_in)[0][:, 2052 + s:2053 + s]], axis=1)),
            "dcw_p": np.ascontiguousarray(dn_convT[:, [s, 4 + s, 8 + s], :]),
            "dsm_p": np.array([[f32(dn_A_log)[0, s], f32(dn_dt_bias)[0, s]]], np.float32),
            "odn_idx": _odn_idx(s),
        })
        in_maps.append(m)
    import os
    _tr = os.environ.get('DBG_TRACE') == '1'
    res = run_bass_kernel_spmd(nc, in_maps, core_ids=list(range(8)), **({'trace': True} if _tr else {}))
    if _tr:
        print('EXEC_NS', res.exec_time_ns, flush=True)
    R = res.results
    y_prompt = np.stack([np.concatenate([R[b * 4 + s]["y_seg"] for s in range(4)], axis=0) for b in range(2)])
    y_sample = np.concatenate([R[c]["y_s"].reshape(16, 8, D) for c in range(8)], axis=0)
    p_dn_conv = np.stack([np.stack([R[b * 4 + s]["pdc"] for s in range(4)], axis=2).reshape(3, 1536)
                          for b in range(2)])[None]
    s_dn_conv = np.concatenate([R[c]["sdc"] for c in range(8)], axis=0)[None]
    p_dn_ssm = np.stack([np.stack([R[b * 4 + s]["pss"] for s in range(4)]) for b in range(2)])[None]
    s_dn_ssm = np.concatenate([R[c]["sss"] for c in range(8)], axis=0)[None]
    p_swa_k = np.stack([R[b * 4 + 3]["psk"].reshape(128, 2, 64) for b in range(2)])[None]
    s_swa_k = np.concatenate([R[c]["ssk"].reshape(16, 128, 2, 64) for c in range(8)], axis=0)[None]
    p_swa_v = np.stack([R[b * 4 + 3]["psv"].reshape(128, 2, 64) for b in range(2)])[None]
    s_swa_v = np.concatenate([R[c]["ssv"].reshape(16, 128, 2, 64) for c in range(8)], axis=0)[None]
    p_ffn_conv = np.stack([R[b * 4 + 3]["pfc"] for b in range(2)])[None]
    s_ffn_conv = np.concatenate([R[c]["sfc"] for c in range(8)], axis=0)[None]
    outs = (y_prompt, y_sample, p_dn_conv, s_dn_conv, p_dn_ssm, s_dn_ssm, p_swa_k, s_swa_k,
            p_swa_v, s_swa_v, p_ffn_conv, s_ffn_conv)
    return tuple(np.ascontiguousarray(o, dtype=np.float32) for o in outs)
```

```python
import contextlib
import concourse.bass as bass
import concourse.mybir as mybir

F32 = mybir.dt.float32
BF16 = mybir.dt.bfloat16
I32 = mybir.dt.int32
AF = mybir.ActivationFunctionType
ALU = mybir.AluOpType
AX = mybir.AxisListType

ENGS = ("pe", "act", "dve", "pool", "sp")
NSLOT = 12


class Tok:
    __slots__ = ("name", "w", "r", "excl")

    def __init__(self, name, excl=False):
        self.name = name
        self.w = None
        self.r = []
        self.excl = excl


class Op:
    __slots__ = ("eng", "idx", "fn", "deps", "dma", "slot", "slotval", "sig", "sigval")

    def __init__(self, eng, idx, fn, dma):
        self.eng = eng
        self.idx = idx
        self.fn = fn
        self.deps = []
        self.dma = dma
        self.slot = None
        self.slotval = None
        self.sig = False
        self.sigval = None


class Buf:
    def __init__(self, t, name, excl=False):
        self.t = t
        self.name = name
        self.toks = {}
        self.excl = excl

    def tok(self, key=None):
        if self.excl:
            key = None
        tk = self.toks.get(key)
        if tk is None:
            tk = Tok(f"{self.name}:{key}", self.excl)
            self.toks[key] = tk
        return tk

    def __getitem__(self, sl):
        return V(self.t[sl], [self.tok(None)])

    def v(self, sl, key):
        if not isinstance(key, (list, tuple)):
            key = [key]
        return V(self.t[sl], [self.tok(k) for k in key])


class V:
    __slots__ = ("ap", "toks")

    def __init__(self, ap, toks):
        self.ap = ap
        self.toks = toks

    def __getitem__(self, sl):
        return V(self.ap[sl], self.toks)

    def re(self, pat, **kw):
        return V(self.ap.rearrange(pat, **kw), self.toks)

    def bc(self, shape):
        return V(self.ap.broadcast_to(shape), self.toks)


def _ap(x):
    return x.ap if isinstance(x, V) else x


class Prog:
    def __init__(self, nc, same_eng_sync=True):
        self.nc = nc
        self.q = {e: [] for e in ENGS}
        self.ndma = {e: 0 for e in ENGS}
        self.slot_last = {e: [None] * NSLOT for e in ENGS}
        self.same = same_eng_sync
        self.es = contextlib.ExitStack()
        self.nbuf = 0

    def sbuf(self, shape, dtype, name=None):
        self.nbuf += 1
        name = name or f"sb{self.nbuf}"
        t = self.es.enter_context(self.nc.sbuf_tensor(name, list(shape), dtype))
        return Buf(t, name)

    def psum(self, shape, dtype, name=None):
        self.nbuf += 1
        name = name or f"ps{self.nbuf}"
        t = self.es.enter_context(self.nc.psum_tensor(name, list(shape), dtype))
        return Buf(t, name, excl=True)

    def dram(self, name, shape, dtype, kind):
        t = self.nc.dram_tensor(name, list(shape), dtype, kind=kind)
        return Buf(t.ap(), name)

    def init_arena(self, nbytes):
        self.arena = self.es.enter_context(self.nc.sbuf_tensor("arena", [128, nbytes // 4], F32))
        self.aoff = 0
        self.asize = nbytes
        self.bar = None
        self.bar_seen = set()

    def alloc(self, shape, dtype, name=None):
        self.nbuf += 1
        name = name or f"a{self.nbuf}"
        free = 1
        for d in shape[1:]:
            free *= d
        esz = 2 if dtype == BF16 else 4
        nb = (free * esz + 31) // 32 * 32
        off = self.aoff
        assert off + nb <= self.asize, f"arena overflow {name} {off}+{nb}>{self.asize}"
        self.aoff = off + nb
        ap = self.arena[0:shape[0], off // 4:(off + nb) // 4]
        if dtype != F32:
            ap = ap.bitcast(dtype)
        ap = ap[:, 0:free]
        if len(shape) == 3:
            ap = ap.rearrange("p (a b) -> p a b", b=shape[2])
        elif len(shape) == 4:
            ap = ap.rearrange("p (a b c) -> p a b c", b=shape[2], c=shape[3])
        return Buf(ap, name)

    def mark(self):
        return self.aoff

    def release(self, mark):
        self.aoff = mark
        self.barrier()

    def barrier(self):
        bar = []
        for e in ENGS:
            if self.q[e]:
                bar.append(self.q[e][-1])
            for s in self.slot_last[e]:
                if s is not None:
                    bar.append(s)
        self.bar = bar
        self.bar_seen = set()

    def op(self, eng, fn, reads=(), writes=(), dma=False):
        q = self.q[eng]
        o = Op(eng, len(q), fn, dma)
        if getattr(self, "bar", None) and eng not in self.bar_seen:
            self.bar_seen.add(eng)
            o.deps.extend(d for d in self.bar if d is not o)
        rt, wt = [], []
        for x in reads:
            if isinstance(x, V):
                for tk in x.toks:
                    (wt if tk.excl else rt).append(tk)
            elif isinstance(x, Tok):
                (wt if x.excl else rt).append(x)
        for x in writes:
            if isinstance(x, V):
                wt.extend(x.toks)
            elif isinstance(x, Tok):
                wt.append(x)
        deps = []
        for t in rt:
            if t.w is not None:
                deps.append(t.w)
        for t in wt:
            if t.w is not None:
                deps.append(t.w)
            deps.extend(t.r)
        for d in deps:
            if d is o:
                continue
            if d.eng == eng and not d.dma:
                if eng == "pe" or eng == "sp" or not self.same:
                    continue
                if d.idx < len(q) - 1:
                    continue
            o.deps.append(d)
        if dma:
            n = self.ndma[eng]
            self.ndma[eng] = n + 1
            o.slot = n % NSLOT
            o.slotval = 16 * (n // NSLOT + 1)
            prev = self.slot_last[eng][o.slot]
            if prev is not None:
                o.deps.append(prev)
            self.slot_last[eng][o.slot] = o
        for t in rt:
            t.r.append(o)
        for t in wt:
            t.w = o
            t.r = []
        q.append(o)
        return o

    def dma(self, out, in_, eng="sp", **kw):
        return self.op(eng, lambda e: e.dma_start(out=_ap(out), in_=_ap(in_), **kw),
                       reads=[in_], writes=[out], dma=True)

    def mm(self, out, lhsT, rhs, start=True, stop=True, **kw):
        return self.op("pe", lambda e: e.matmul(_ap(out), _ap(lhsT), _ap(rhs), start=start, stop=stop, **kw),
                       reads=[lhsT, rhs], writes=[out])

    def tr(self, out, in_, ident):
        if _ap(in_).dtype == F32:
            return self.mm(out, in_, ident)
        return self.op("pe", lambda e: e.transpose(_ap(out), _ap(in_), _ap(ident)),
                       reads=[in_, ident], writes=[out])

    def act(self, out, in_, func, bias=None, scale=None, accum_out=None, eng="act"):
        kw = {}
        rd = [in_]
        wr = [out]
        if bias is not None:
            kw["bias"] = _ap(bias)
            rd.append(bias)
        if scale is not None:
            kw["scale"] = _ap(scale)
            rd.append(scale)
        if accum_out is not None:
            kw["accum_out"] = _ap(accum_out)
            wr.append(accum_out)
        return self.op(eng, lambda e: e.activation(_ap(out), _ap(in_), func, **kw), reads=rd, writes=wr)

    def tt(self, out, in0, in1, op, eng="dve"):
        return self.op(eng, lambda e: e.tensor_tensor(_ap(out), _ap(in0), _ap(in1), op),
                       reads=[in0, in1], writes=[out])

    def ts(self, out, in0, s1, op0, s2=None, op1=None, accum_out=None, eng="dve"):
        rd = [in0, s1, s2]
        wr = [out, accum_out]
        kw = {}
        if op1 is not None:
            kw["op1"] = op1
        if accum_out is not None:
            kw["accum_out"] = _ap(accum_out)
        return self.op(eng, lambda e: e.tensor_scalar(_ap(out), _ap(in0), _ap(s1), _ap(s2), op0, **kw),
                       reads=rd, writes=wr)

    def stt(self, out, in0, scalar, in1, op0, op1, eng="dve"):
        return self.op(eng, lambda e: e.scalar_tensor_tensor(_ap(out), _ap(in0), _ap(scalar), _ap(in1), op0, op1),
                       reads=[in0, scalar, in1], writes=[out])

    def copy(self, out, in_, eng="dve"):
        if eng == "act":
            return self.op(eng, lambda e: e.copy(_ap(out), _ap(in_)), reads=[in_], writes=[out])
        return self.op(eng, lambda e: e.tensor_copy(_ap(out), _ap(in_)), reads=[in_], writes=[out])

    def memset(self, out, val, eng="dve"):
        return self.op(eng, lambda e: e.memset(_ap(out), val), writes=[out])

    def reduce(self, out, in_, op, axis=AX.X, eng="dve"):
        return self.op(eng, lambda e: e.tensor_reduce(_ap(out), _ap(in_), axis, op), reads=[in_], writes=[out])

    def recip(self, out, in_, eng="dve"):
        return self.op(eng, lambda e: e.reciprocal(_ap(out), _ap(in_)), reads=[in_], writes=[out])

    def generic(self, eng, fn, reads=(), writes=()):
        return self.op(eng, fn, reads=reads, writes=writes)

    def emit(self, final_waits=()):
        nc = self.nc
        for e in ENGS:
            for o in self.q[e]:
                for d in o.deps:
                    if not d.dma:
                        d.sig = True
        final = []
        for e in ENGS:
            if self.q[e]:
                last = self.q[e][-1]
                if not last.dma:
                    last.sig = True
                final.append(last)
                for s in self.slot_last[e]:
                    if s is not None:
                        final.append(s)
        for e in ENGS:
            c = 0
            for o in self.q[e]:
                if o.sig:
                    c += 1
                    o.sigval = c
        print('SIGCOUNTS', {e: sum(1 for o in self.q[e] if o.sig) for e in ENGS}, {e: len(self.q[e]) for e in ENGS}, flush=True)
        es = self.es
        csem = {e: es.enter_context(nc.semaphore(f"c_{e}")) for e in ENGS}
        dsem = {e: [es.enter_context(nc.semaphore(f"d_{e}_{i}")) for i in range(NSLOT)]
                for e in ENGS if self.ndma[e] > 0}
        block = es.enter_context(nc.Block())
        engobj = {"pe": block.tensor, "act": block.scalar, "dve": block.vector,
                  "pool": block.gpsimd, "sp": block.sync}

        def make(ename):
            def body(eng):
                seen = {}

                def wait_for(d):
                    if d.dma:
                        key = ("d", d.eng, d.slot)
                        val = d.slotval
                        sem = dsem[d.eng][d.slot]
                    else:
                        key = ("c", d.eng)
                        val = d.sigval
                        sem = csem[d.eng]
                    if seen.get(key, 0) >= val:
                        return
                    seen[key] = val
                    eng.wait_ge(sem, val)

                for o in self.q[ename]:
                    for d in o.deps:
                        wait_for(d)
                    ins = o.fn(eng)
                    if o.dma:
                        ins.then_inc(dsem[ename][o.slot], 16)
                    elif o.sig:
                        ins.then_inc(csem[ename], 1)
                    if not o.dma:
                        seen[("c", ename)] = max(seen.get(("c", ename), 0), 0)
                if ename == "sp":
                    for d in final:
                        wait_for(d)
            return body

        for e in ENGS:
            if self.q[e] or e == "sp":
                engobj[e](make(e))

    def close(self):
        self.es.close()


import math
import numpy as np
from concourse.bass_utils import run_bass_kernel_spmd

D = 1024
SEQ = 8192
NT_SEG = 18
EPS = 1e-6
NEG = -1e30
DFF = 2816
NFC = 44


def build(phases=("dn", "dns", "tok", "toks", "ffn", "ffns"), nchunks=64, ntiles=NT_SEG):
    nc = bass.Bass("TRN2", target_bir_lowering=False)
    P = Prog(nc)
    di = lambda n, s, dt=F32: P.dram(n, s, dt, "ExternalInput")
    do = lambda n, s, dt=F32: P.dram(n, s, dt, "ExternalOutput")
    xfull = di("xfull", [SEQ, D])
    xseg = di("xseg", [NT_SEG * 128, D])
    xsam = di("xsam", [128, D])
    cexp = di("cexp", [2, 128, D])
    w_ada = di("w_ada", [D, 6 * D])
    b_ada = di("b_ada", [1, 6 * D])
    w_in = di("w_in", [D, 2824])
    w_out = di("w_out", [D, D])
    w_up = di("w_up", [D, 5632])
    w_down = di("w_down", [DFF, D])
    normw = di("normw", [3, D])
    dn_convT = di("dn_convT", [128, 12, 4])
    ffn_convT = di("ffn_convT", [128, NFC, 4])
    dn_small = di("dn_small", [1, 136])
    sinks = di("sinks", [1, 8])
    rel_bias = di("rel_bias", [32, 8])
    cst = di("cst", [128, 7, 128])
    oh = di("oh", [32, 384])
    swa_mask = di("swa_mask", [128, 256])
    halo_mask = di("halo_mask", [128, 256])
    keep = di("keep", [128, 1])
    st_conv = di("st_conv", [48, 1536])
    st_ssm = di("st_ssm", [16, 4, 128, 128])
    ck = di("ck", [16, 128, 128])
    cv = di("cv", [16, 128, 128])
    st_ffn = di("st_ffn", [32, 5632])
    odn_idx = di("odn_idx", [128, 17 * 4], I32)
    w_dnp = di("w_dnp", [D, 514])
    dcw_p = di("dcw_p", [128, 3, 4])
    dsm_p = di("dsm_p", [1, 2])

    y_seg = do("y_seg", [2048, D])
    y_s = do("y_s", [128, D])
    pdc = do("pdc", [3, 3, 128])
    sdc = do("sdc", [16, 3, 1536])
    pss = do("pss", [128, 128])
    sss = do("sss", [16, 4, 128, 128])
    psk = do("psk", [128, 128])
    ssk = do("ssk", [16, 128, 128])
    psv = do("psv", [128, 128])
    ssv = do("ssv", [16, 128, 128])
    pfc = do("pfc", [2, 5632])
    sfc = do("sfc", [16, 2, 5632])

    odn = P.dram("odn", [4, 4, 2048, 128], BF16, "Internal")
    odn_mine = P.dram("odn_mine", [SEQ, 128], BF16, "Internal")
    odn_s = P.dram("odn_s", [4, 128, 128], BF16, "Internal")
    xmid = P.dram("xmid", [17 * 128, D], F32, "Internal")
    xmid_s = P.dram("xmid_s", [128, D], F32, "Internal")
    btab = P.dram("btab", [8, 384], F32, "Internal")
    mod2_d = P.dram("mod2_d", [2, 128, 3, D], F32, "Internal")
    oswa_s = P.dram("oswa_s", [16, 8, 512], BF16, "Internal")

    P.init_arena(206 * 1024)
    pb = [P.psum([128, 512], F32, f"pb{i}") for i in range(8)]

    def pq(k, q, rows=128, cols=128):
        return pb[k].v((slice(0, rows), slice(q * 128, q * 128 + cols)), q)

    def pbf(k):
        return V(pb[k].t[:, :].bitcast(BF16), [pb[k].tok(q) for q in range(4)])

    def pfull(k, lo=0, hi=512):
        return V(pb[k].t[:, lo:hi], [pb[k].tok(q) for q in range(lo // 128, (hi + 127) // 128)])

    C_ = P.alloc([128, 7, 128], F32, "cst")
    P.dma(C_[:, :, :], cst[:, :, :])
    ident = C_[:, 0, :]
    maskSL = C_[:, 1, :]
    maskUI = C_[:, 2, :]
    Jm = C_[:, 3, :]
    ones = C_[:, 4, :]
    bigSL = C_[:, 5, :]
    nbigUI = C_[:, 6, :]
    identb_ = P.alloc([128, 128], BF16, "identb")
    P.copy(identb_[:, :], ident)
    identb = identb_[:, :]
    epsb = P.alloc([128, 1], F32, "eps")
    P.memset(epsb[:, :], EPS)
    oneb = P.alloc([128, 1], F32, "one")
    P.memset(oneb[:, :], 1.0)
    nwf = P.alloc([128, D], F32, "nwf")
    P.dma(nwf[:, :], V(normw.t[2:3, :].broadcast_to([128, D]), [normw.tok()]))
    dsm = P.alloc([128, 136], F32, "dsm")
    P.dma(dsm[:, :], V(dn_small.t[0:1, :].broadcast_to([128, 136]), [dn_small.tok()]))
    negA = P.alloc([128, 4], F32, "negA")
    P.act(negA[:, :], dsm[:, 0:4], AF.Exp)
    P.ts(negA[:, :], negA[:, :], -1.0, ALU.mult)
    dtb = dsm[:, 4:8]
    dnw = dsm[:, 8:136]
    snk = P.alloc([128, 8], F32, "snk")
    P.dma(snk[:, :], V(sinks.t[0:1, :].broadcast_to([128, 8]), [sinks.tok()]))
    dcw = P.alloc([128, 12, 4], F32, "dcw")
    P.dma(dcw[:, :, :], dn_convT[:, :, :])
    fcw = P.alloc([128, NFC, 4], F32, "fcw")
    P.dma(fcw[:, :, :], ffn_convT[:, :, :])
    keep_t = P.alloc([128, 1], F32, "keep")
    P.dma(keep_t[:, :], keep[:, :])
    mQ = P.mark()
    MOD1 = [P.alloc([128, 3, D], F32, f"mod1_{g}") for g in range(2)]
    mW = P.mark()
    MOD2 = [P.alloc([128, 3, D], F32, f"mod2_{g}") for g in range(2)]
    nw = P.alloc([128, 2, D], F32, "nw")
    for i in range(2):
        P.dma(nw[:, i, :], V(normw.t[i:i + 1, :].broadcast_to([128, D]), [normw.tok()]))

    scT = P.alloc([128, 2, 8, 128], BF16, "scT")
    for g in range(2):
        ct = P.alloc([128, D], F32, f"ct{g}")
        cb = P.alloc([128, D], BF16, f"cb{g}")
        P.dma(ct[:, :], cexp[g])
        P.act(cb[:, :], ct[:, :], AF.Silu)
        for kt in range(8):
            P.tr(pbf(3 + g)[:, kt * 128:(kt + 1) * 128], cb[:, kt * 128:(kt + 1) * 128], identb)
        P.copy(scT[:, g, :, :].re("p a b -> p (a b)"), pbf(3 + g))
    wab = [P.alloc([128, 8, 512], BF16, f"wab{i}") for i in range(2)]
    bab = [P.alloc([128, 512], F32, f"bab{i}") for i in range(2)]
    mtmp = [P.alloc([128, 512], F32, f"mtmp{i}") for i in range(2)]
    for j in range(12):
        wb = wab[j % 2]
        bb = bab[j % 2]
        P.dma(wb[:, :, :], V(w_ada.t[:, j * 512:(j + 1) * 512].rearrange("(kt p) n -> p kt n", p=128), [w_ada.tok()]),
              eng="pool")
        P.dma(bb[:, :], V(b_ada.t[0:1, j * 512:(j + 1) * 512].broadcast_to([128, 512]), [b_ada.tok()]))
        comp, half = j // 2, j % 2
        for g in range(2):
            ps = pfull(5 + g)
            for kt in range(8):
                P.mm(ps, scT[:, g, kt, :], wb[:, kt, :], start=(kt == 0), stop=(kt == 7))
            dst = (MOD1 if comp < 3 else MOD2)[g][:, comp % 3, half * 512:(half + 1) * 512]
            if comp % 3 == 1:
                t = mtmp[g]
                P.tt(t[:, :], ps, bb[:, :], ALU.add)
                P.stt(dst, t[:, :], 1.0, nw[:, 0 if comp < 3 else 1, half * 512:(half + 1) * 512], ALU.add, ALU.mult)
            else:
                P.tt(dst, ps, bb[:, :], ALU.add)
    for g in range(2):
        P.dma(mod2_d[g], MOD2[g][:, :, :])
    P.release(mW)

    def norm_mod_T(xt, A, SH, hT, wk):
        junk, ssq, hm, hb = wk["junk"], wk["ssq"], wk["hm"], wk["hb"]
        P.act(junk[:, :], xt, AF.Square, accum_out=ssq[:, :])
        P.ts(ssq[:, :], ssq[:, :], 1.0 / D, ALU.mult, EPS, ALU.add)
        P.act(ssq[:, :], ssq[:, :], AF.Ln)
        P.act(ssq[:, :], ssq[:, :], AF.Exp, scale=-0.5)
        P.stt(hm[:, :], xt, ssq[:, 0:1], A, ALU.mult, ALU.mult)
        P.tt(hb[:, :], hm[:, :], SH, ALU.add)
        bank = wk["tbank"]
        for kt in range(8):
            P.tr(pbf(bank)[:, kt * 128:(kt + 1) * 128], hb[:, kt * 128:(kt + 1) * 128], identb)
        hTv = hT[:, :, :] if isinstance(hT, Buf) else hT
        P.copy(hTv, pbf(bank).re("p (a b) -> p a b", b=128), eng="act")

    def load_w(dst, src_ap, tok):
        P.dma(dst, V(src_ap.rearrange("(kt p) n -> p kt n", p=128), [tok]), eng="pool")

    class Chain:
        def __init__(self, b0, b1):
            self.banks = (b0, b1)
            self.n = 0

        def ps(self, rows=128, cols=128):
            n = self.n
            self.n += 1
            bank = self.banks[n % 2]
            q = (n // 2) % 4
            return pb[bank].v((slice(0, rows), slice(q * 128, q * 128 + cols)), q)

    def dn_chunk(C, L, nA, W, qT, kT, vT, S_in, S_out, o_dst, ch, s_ready=None, s_done=None, zsv=None, betav=None, eav=None):
        c = slice(0, C)
        ps = ch.ps
        zsv = W["zs"][:, c] if zsv is None else zsv
        betav = W["beta"][:, c] if betav is None else betav
        eav = W["ea"][:, c] if eav is None else eav
        sq, rn = W["sq"], W["rn"]
        P.tt(sq[:, 0, c], qT, qT, ALU.mult)
        P.tt(sq[:, 1, c], kT, kT, ALU.mult)
        for i in range(2):
            sp_ = ps(128, C)
            P.mm(sp_, ones, sq[:, i, c])
            P.act(rn[:, i, c], sp_, AF.Ln, bias=epsb[:, 0:1])
        P.act(rn[:, :, c], rn[:, :, c], AF.Exp, scale=-0.5)
        yield
        gbc = W["gbc"]
        P.act(gbc[:, c], eav, AF.Ln, bias=oneb[:, 0:1])
        P.ts(gbc[:, c], gbc[:, c], nA, ALU.mult)
        gp = ps(C)
        P.mm(gp, gbc[:, c], ident)
        gtr = W["gtr"]
        P.copy(gtr[c, :], gp, eng="act")
        yield
        Gbc = ps(128, C)
        P.mm(Gbc, gtr[c, :], maskUI[c, c])
        Gtk = ps(C)
        P.mm(Gtk, maskUI[c, c], gtr[c, :])
        eG, glast, cdec = W["eG"], W["glast"], W["cdec"]
        P.act(eG[:, c], Gbc, AF.Exp)
        P.copy(glast[:, :], Gbc[:, C - 1:C], eng="act")
        P.act(cdec[:, :], glast[:, :], AF.Exp)
        tks = W["tks"]
        P.act(tks[c, 0:1], Gtk[:, 0:1], AF.Exp)
        P.act(tks[c, 2:3], Gtk[:, 0:1], AF.Exp, scale=-1.0, bias=glast[c, 0:1])
        gtk, Dm, DTm = W["gtk"], W["kp"], W["km"]
        P.copy(gtk[c, 0:1], Gtk[:, 0:1], eng="act")
        P.stt(Dm[c, c], Gbc[c, :], gtk[c, 0:1], bigSL[c, c], ALU.subtract, ALU.max)
        P.stt(DTm[c, c], Gbc[c, :], gtk[c, 0:1], nbigUI[c, c], ALU.subtract, ALU.min)
        P.act(Dm[c, c], Dm[c, c], AF.Exp, scale=-1.0)
        P.act(DTm[c, c], DTm[c, c], AF.Exp)
        yield
        def tposed(src):
            t = ps(C)
            if _ap(src).dtype == BF16:
                tb = V(_ap(t).bitcast(BF16)[:, 0:128], t.toks)
                P.tr(tb, src, identb)
                return tb
            P.mm(t, src, ident)
            return t
        bp = tposed(betav)
        P.copy(tks[c, 3:4], bp[:, 0:1], eng="act")
        P.tt(tks[c, 1:2], tks[c, 0:1], tks[c, 3:4], ALU.mult)
        qp, kn, qn = W["qp"], W["kn"], W["enG"]
        P.stt(qn[:, c], qT, 128.0 ** -0.5, rn[:, 0, c], ALU.mult, ALU.mult)
        P.tt(qp[:, c], qn[:, c], eG[:, c], ALU.mult)
        P.tt(kn[:, c], kT, rn[:, 1, c], ALU.mult)
        yield
        bkp, ktl, bv, zt = W["bkp"], W["ktl"], W["bv"], W["zt"]
        knp = ps(C)
        P.mm(knp, kn[:, c], ident)
        P.act(bkp[c, :], knp, AF.Identity, scale=tks[c, 1:2])
        P.act(ktl[c, :], knp, AF.Identity, scale=tks[c, 2:3])
        vp = tposed(vT)
        P.act(bv[c, :], vp, AF.Identity, scale=tks[c, 3:4])
        zp = tposed(zsv)
        P.tt(zt[c, :], zp, dnw[c, :], ALU.mult)
        yield
        A, AT, QKT, X = W["A"], W["AT"], W["QKT"], W["X"]
        Ap = ps(C, C)
        P.mm(Ap, kn[:, c], kn[:, c])
        P.stt(A[c, c], Ap, tks[c, 3:4], Dm[c, c], ALU.mult, ALU.mult)
        Qp = ps(C, C)
        P.mm(Qp, kn[:, c], qn[:, c])
        P.tt(QKT[c, c], Qp, DTm[c, c], ALU.mult)
        yield
        ATp = ps(C, C)
        P.mm(ATp, A[c, c], ident[c, c])
        P.copy(AT[c, c], ATp, eng="act")
        P.tt(X[c, c], ident[c, c], AT[c, c], ALU.subtract)
        yield
        Pm, PT = A, AT
        for lv in range(L):
            Pn, PTn = W["P"][lv % 2], W["PT"][lv % 2]
            p1 = ps(C, C)
            P.mm(p1, PT[c, c], Pm[c, c])
            P.copy(Pn[c, c], p1, eng="act")
            if lv < L - 1:
                p2 = ps(C, C)
                P.mm(p2, Pm[c, c], PT[c, c])
                P.copy(PTn[c, c], p2)
            yield
            p3 = ps(C, C)
            P.mm(p3, Pn[c, c], X[c, c])
            P.tt(X[c, c], X[c, c], p3, ALU.add)
            Pm, PT = Pn, PTn
            yield
        nwk, u = W["nwk"], W["u"]
        wp = ps(128, C)
        P.mm(wp, bkp[c, :], X[c, c])
        P.act(nwk[:, c], wp, AF.Identity, scale=-1.0)
        yield
        while s_ready is not None and not s_ready():
            yield
        up_ = ps(C)
        P.mm(up_, X[c, c], bv[c, :], start=True, stop=False)
        P.mm(up_, nwk[:, c], S_in, start=False, stop=True)
        P.copy(u[c, :], up_)
        yield
        op_ = ps(C)
        P.mm(op_, qp[:, c], S_in, start=True, stop=False)
        P.mm(op_, QKT[c, c], u[c, :], start=False, stop=True)
        sp2 = ps()
        P.mm(sp2, ktl[c, :], u[c, :])
        P.stt(S_out, S_in, cdec[:, 0:1], sp2, ALU.mult, ALU.add)
        if s_done is not None:
            s_done()
        oss, ojunk, of = W["oss"], W["ojunk"], W["of"]
        P.act(ojunk[c, :], op_, AF.Square, accum_out=oss[c, :])
        yield
        P.ts(oss[c, :], oss[c, :], 1.0 / 128, ALU.mult, EPS, ALU.add)
        P.act(oss[c, :], oss[c, :], AF.Ln)
        P.act(oss[c, :], oss[c, :], AF.Exp, scale=-0.5)
        P.stt(of[c, :], op_, oss[c, 0:1], zt[c, :], ALU.mult, ALU.mult)
        P.dma(o_dst, of[c, :])
        yield

    def run_window(factories, width, slotted=False):
        pending = list(factories)
        active = []
        free = list(range(width))
        while pending or active:
            while pending and free:
                sl = free.pop(0)
                f = pending.pop(0)
                active.append((sl, f(sl) if slotted else f()))
            nxt = []
            for sl, g in active:
                try:
                    next(g)
                    nxt.append((sl, g))
                except StopIteration:
                    free.append(sl)
            active = nxt

    def run_multi(queues):
        state = [{"pending": list(f), "free": list(range(n)), "active": []} for f, n in queues]
        while any(q["pending"] or q["active"] for q in state):
            for q in state:
                while q["pending"] and q["free"]:
                    sl = q["free"].pop(0)
                    q["active"].append((sl, q["pending"].pop(0)(sl)))
                nxt = []
                for sl, g in q["active"]:
                    try:
                        next(g)
                        nxt.append((sl, g))
                    except StopIteration:
                        q["free"].append(sl)
                q["active"] = nxt

    def run_interleaved(gens):
        gens = list(gens)
        while gens:
            nxt = []
            for g in gens:
                try:
                    next(g)
                    nxt.append(g)
                except StopIteration:
                    pass
            gens = nxt

    def dn_wset(i):
        W = {}
        f = lambda n, s, dt=F32: P.alloc(s, dt, f"dn{i}_{n}")
        W["sq"] = f("sq", [128, 2, 128]); W["rn"] = f("rn", [128, 2, 128])
        for n in ("gbc", "gtr", "eG", "enG", "qp", "kn", "kp", "km", "bkp", "ktl", "bv", "zt", "A", "AT", "QKT", "X",
                  "nwk", "u", "ojunk", "zs", "beta", "ea"):
            W[n] = f(n, [128, 128])
        W["P"] = [f("P0", [128, 128]), f("P1", [128, 128])]
        W["PT"] = [f("PT0", [128, 128]), f("PT1", [128, 128])]
        W["glast"] = f("glast", [128, 1]); W["cdec"] = f("cdec", [128, 1]); W["tks"] = f("tks", [128, 4])
        W["gtk"] = f("gtk", [128, 1])
        W["oss"] = f("oss", [128, 1]); W["of"] = f("of", [128, 128], BF16)
        return W

    if "dn" in phases or "dns" in phases:
        Ws = [dn_wset(i) for i in range(4)]
        _hm = P.alloc([128, D], F32, "hm")
        xw = {"junk": _hm, "ssq": P.alloc([128, 1], F32, "ssq"),
              "hm": _hm, "hb": P.alloc([128, D], BF16, "hb"), "tbank": 0}
        xts = [P.alloc([128, D], F32, "xt0")]
        hTs = [P.alloc([128, 8, 128], BF16, "hT0")]
        mS = P.mark()
        if "dn" in phases:
            xts.append(P.alloc([128, D], F32, "xt1"))
            hT4 = [P.alloc([128, 8, 512], BF16, f"hT4_{i}") for i in range(2)]
            rawg = [[P.alloc([128, 515], F32, f"rawg{i}_{g}") for g in range(3)] for i in range(2)]
            cvg = [[P.alloc([128, 512], F32, f"cvg{i}_{g}") for g in range(3)] for i in range(2)]
            zsg = [P.alloc([128, 512], BF16, f"zsg{i}") for i in range(2)]
            betag = [P.alloc([128, 512], BF16, f"betag{i}") for i in range(2)]
            betaf = P.alloc([128, 512], F32, "betaf")
            cvgv = [P.alloc([128, 512], BF16, f"cvgv{i}") for i in range(2)]
            eag = [P.alloc([128, 512], F32, f"eag{i}") for i in range(2)]
            wdp = [P.alloc([128, 8, 128], BF16, f"wdp{g}") for g in range(6)]
            wbap = P.alloc([128, 8, 2], F32, "wbap")
            P.dma(wbap[:, :, :], V(w_dnp.t[:, 512:514].rearrange("(kt p) n -> p kt n", p=128), [w_dnp.tok()]))
            for g in range(4):
                load_w(wdp[g][:, :, :], w_dnp.t[:, g * 128:(g + 1) * 128], w_dnp.tok())
            for g in range(2):
                P.copy(wdp[4 + g][:, :, :], wbap[:, :, g:g + 1].bc([128, 8, 128]))
            dcwp = P.alloc([128, 3, 4], F32, "dcwp")
            P.dma(dcwp[:, :, :], dcw_p[:, :, :])
            dsp = P.alloc([128, 2], F32, "dsp")
            P.dma(dsp[:, :], V(dsm_p.t[0:1, :].broadcast_to([128, 2]), [dsm_p.tok()]))
            negAp = P.alloc([128, 1], F32, "negAp")
            P.act(negAp[:, :], dsp[:, 0:1], AF.Exp)
            P.ts(negAp[:, :], negAp[:, :], -1.0, ALU.mult)
            Sbuf = [P.alloc([128, 128], F32, f"S_{i}") for i in range(2)]
            pdct = P.alloc([3, 3, 128], F32, "pdct")

        chains = [Chain(0, 1), Chain(2, 3), Chain(4, 5), Chain(6, 7)]

        def dn_proj(wl, hT, ncol, ch):
            outs = []
            for g in range(6):
                dst = ch.ps(128, ncol)
                for kt in range(8):
                    P.mm(dst, wl[g][:, kt, :], hT[:, kt, 0:ncol], start=(kt == 0), stop=(kt == 7))
                outs.append(dst)
            return outs

        if "dn" in phases:
            P.memset(Sbuf[0][:, :], 0.0)
            for g in range(3):
                P.memset(rawg[0][g][:, 0:3], 0.0)
            flags = {}
            ngroups = nchunks // 4
            cchains = [Chain(2, 3), Chain(4, 5), Chain(6, 7)]

            def group_gen(j, _slot):
                p = j % 2
                hT4_, rg, cg = hT4[p], rawg[p], cvg[p]
                while j >= 2 and not all(flags.get(("Cdone", ci)) for ci in range((j - 2) * 4, (j - 1) * 4)):
                    yield
                for i in range(4):
                    ci = j * 4 + i
                    xt = xts[i % 2]
                    P.dma(xt[:, :], xfull[ci * 128:(ci + 1) * 128, :])
                    norm_mod_T(xt[:, :], MOD1[0][:, 1, :], MOD1[0][:, 0, :], hT4_[:, :, i * 128:(i + 1) * 128], xw)
                    yield
                for g in range(6):
                    ps = pfull(1)
                    for kt in range(8):
                        P.mm(ps, wdp[g][:, kt, :], hT4_[:, kt, :], start=(kt == 0), stop=(kt == 7))
                    if g < 3:
                        P.copy(rg[g][:, 3:515], ps, eng="act")
                    elif g == 3:
                        P.act(zsg[p][:, :], ps, AF.Silu)
                    elif g == 4:
                        P.act(betaf[:, :], ps, AF.Exp, scale=-1.0)
                        P.act(betaf[:, :], betaf[:, :], AF.Ln, bias=oneb[:, 0:1])
                        P.act(betag[p][:, :], betaf[:, :], AF.Exp, scale=-1.0)
                    else:
                        P.act(eag[p][:, :], ps, AF.Exp, bias=dsp[:, 1:2])
                    yield
                for g in range(3):
                    r = rg[g]
                    cw = dcwp[:, g, :]
                    o = cg[g]
                    P.copy(rawg[1 - p][g][:, 0:3], r[:, 512:515], eng="pool")
                    P.ts(o[:, :], r[:, 0:512], cw[:, 0:1], ALU.mult)
                    for jj in range(1, 4):
                        P.stt(o[:, :], r[:, jj:jj + 512], cw[:, jj:jj + 1], o[:, :], ALU.mult, ALU.add)
                    P.act((cvgv[p] if g == 2 else o)[:, :], o[:, :], AF.Silu)
                    if j == ngroups - 1:
                        tp = V(pb[1].t[0:3, 0:128], [pb[1].tok()])
                        P.mm(tp, r[:, 512:515], ident)
                        P.copy(pdct[:, g, :], tp)
                    yield
                flags[("G", j)] = True

            def chunk_gen(ci, k):
                j, i = ci // 4, ci % 4
                p = j % 2
                cs = slice(i * 128, (i + 1) * 128)
                W, ch = Ws[k], cchains[k]
                last = (ci == nchunks - 1)
                while not flags.get(("G", j)):
                    yield
                S_in, S_out = Sbuf[ci % 2], Sbuf[(ci + 1) % 2]
                yield from dn_chunk(128, 6, negAp[:, 0:1], W, cvg[p][0][:, cs], cvg[p][1][:, cs], cvgv[p][:, cs],
                                    S_in[:, :], S_out[:, :], odn_mine[ci * 128:(ci + 1) * 128, :], ch,
                                    s_ready=(lambda: ci == 0 or flags.get(("S", ci - 1))),
                                    s_done=(lambda: flags.__setitem__(("S", ci), True)),
                                    zsv=zsg[p][:, cs], betav=betag[p][:, cs], eav=eag[p][:, cs])
                flags[("Cdone", ci)] = True
                if last:
                    P.dma(pss[:, :], S_out[:, :])
                    P.dma(pdc[:, :, :], pdct[:, :, :])

            run_multi([
                ([(lambda k, j=j: group_gen(j, k)) for j in range(ngroups)], 1),
                ([(lambda k, ci=ci: chunk_gen(ci, k)) for ci in range(nchunks)], 3),
            ])
            import os as _os
            for j in range(0 if _os.environ.get('DBG_NOCC') else 4):
                P.op("pool", lambda e, j=j: e.collective_compute(
                    "AllGather", ALU.bypass, replica_groups=[[0, 1, 2, 3], [4, 5, 6, 7]],
                    ins=[odn_mine.t[j * 2048:(j + 1) * 2048, :]],
                    outs=[odn.t[j].rearrange("h t c -> (h t) c")]),
                    reads=[odn_mine[:, :]], writes=[odn[0]])

        P.release(mS)
        if "dns" in phases:
            wdn = [[P.alloc([128, 8, 128], BF16, f"wdn{h}_{g}") for g in range(6)] for h in range(4)]
            wba = P.alloc([128, 8, 8], F32, "wba")
            P.dma(wba[:, :, :], V(w_in.t[:, 2048:2056].rearrange("(kt p) n -> p kt n", p=128), [w_in.tok()]))
            for h in range(4):
                for g in range(4):
                    c0 = g * 512 + h * 128
                    load_w(wdn[h][g][:, :, :], w_in.t[:, c0:c0 + 128], w_in.tok())
                for g in range(2):
                    P.copy(wdn[h][4 + g][:, :, :], wba[:, :, g * 4 + h:g * 4 + h + 1].bc([128, 8, 128]))
            rawtok = P.alloc([128, 1536], F32, "rawtok")
            xt, hT = xts[0], hTs[0]
            P.dma(xt[:, :], xsam[:, :])
            norm_mod_T(xt[:, :], MOD1[1][:, 1, :], MOD1[1][:, 0, :], hT, xw)
            stc_b = P.alloc([128, 1536], F32, "stc")
            stc = stc_b
            P.dma(stc[0:48, :], st_conv[:, :])
            raws = P.alloc([128, 12, 16, 11], F32, "raws")
            cvs = P.alloc([128, 12, 128], F32, "cvs")
            zs_s = P.alloc([128, 4, 128], BF16, "zs_s")
            be_s = P.alloc([128, 4, 128], BF16, "be_s")
            be_f = P.alloc([128, 128], F32, "be_f")
            cvsv = P.alloc([128, 4, 128], BF16, "cvsv")
            ea_s = P.alloc([128, 4, 128], F32, "ea_s")
            import os
            STEP = int(os.environ.get('DBG_STEP', '9'))
            for h in range(4 if STEP >= 2 else 0):
                pr = dn_proj(wdn[h], hT, 128, chains[h])
                for g in range(3 if STEP >= 3 else 0):
                    cc = g * 4 + h
                    tp = chains[h].ps(128, 48)
                    P.mm(tp, stc[0:48, g * 512 + h * 128:g * 512 + (h + 1) * 128], ident[0:48, 0:48])
                    P.copy(raws[:, cc, :, 0:3], tp.re("p (b j) -> p b j", j=3))
                    P.copy(raws[:, cc, :, 3:11], pr[g].re("p (b t) -> p b t", t=8), eng="act")
                    o = cvs[:, cc, :].re("p (b t) -> p b t", t=8)
                    cw = dcw[:, cc, :]
                    P.ts(o, raws[:, cc, :, 0:8], cw[:, 0:1], ALU.mult)
                    for j in range(1, 4):
                        P.stt(o, raws[:, cc, :, j:j + 8], cw[:, j:j + 1], o, ALU.mult, ALU.add)
                    P.act(cvsv[:, h, :] if g == 2 else cvs[:, cc, :], cvs[:, cc, :], AF.Silu)
                P.act(zs_s[:, h, :], pr[3], AF.Silu)
                P.act(be_f[:, :], pr[4], AF.Exp, scale=-1.0)
                P.act(be_f[:, :], be_f[:, :], AF.Ln, bias=oneb[:, 0:1])
                P.act(be_s[:, h, :], be_f[:, :], AF.Exp, scale=-1.0)
                P.act(ea_s[:, h, :], pr[5], AF.Exp, bias=dtb[:, h:h + 1])
            for cc in range(12 if STEP >= 4 else 0):
                rn_ = stc_b[:, cc * 128:(cc + 1) * 128]
                P.copy(rn_.re("p (b t) -> p b t", t=8), raws[:, cc, :, 3:11], eng="pool")
                tp = chains[cc % 4].ps()
                P.mm(tp, rn_, ident)
                g, h = cc // 4, cc % 4
                P.copy(rawtok[:, g * 512 + h * 128:g * 512 + (h + 1) * 128], tp)
            for b in range(16 if STEP >= 5 else 0):
                P.dma(sdc[b], rawtok[b * 8 + 5:b * 8 + 8, :])
            Ss = [P.alloc([128, 128], F32, f"Ss{i}") for i in range(4)]
            So = [P.alloc([128, 128], F32, f"So{i}") for i in range(4)]

            def sgen(b, h, k):
                W = Ws[k]
                S_in, S_out = Ss[k], So[k]
                P.dma(S_in[:, :], st_ssm[b, h])
                cs = slice(b * 8, b * 8 + 8)
                P.copy(W["ea"][:, 0:8], ea_s[:, h, cs], eng="pool")
                yield
                yield from dn_chunk(8, 2, negA[:, h:h + 1], W, cvs[:, h, cs], cvs[:, 4 + h, cs], cvsv[:, h, cs], S_in[:, :],
                                    S_out[:, :], odn_s[h, b * 8:b * 8 + 8, :], chains[k],
                                    zsv=zs_s[:, h, cs], betav=be_s[:, h, cs])
                P.dma(sss[b, h], S_out[:, :])

            run_window([(lambda k, b=b, h=h: sgen(b, h, k)) for b in range(16) for h in range(4)], 4, slotted=True)
        P.release(mW)

    if "tok" in phases or "toks" in phases:
        biasM = P.alloc([128, 8, 256], F32, "biasM")
        biasM0 = P.alloc([128, 8, 256], F32, "biasM0")
        mB = P.mark()
        rb = P.alloc([32, 8], F32, "rb")
        oh_t = P.alloc([32, 384], F32, "oh")
        P.dma(rb[:, :], rel_bias[:, :])
        P.dma(oh_t[:, :], oh[:, :])
        P.mm(pfull(0, 0, 384)[0:8, :], rb[:, :], oh_t[:, :])
        gt = P.alloc([8, 384], F32, "gt")
        P.copy(gt[:, :], pfull(0, 0, 384)[0:8, :])
        P.dma(btab[:, :], gt[:, :])
        h2 = P.alloc([128, 8, 256], F32, "h2")
        P.dma(h2[:, :, :], V(bass.AP(btab.t.tensor, 0, [[1, 128], [384, 8], [1, 256]]), [btab.tok()]))
        smask = P.alloc([128, 256], F32, "smask")
        P.dma(smask[:, :], swa_mask[:, :])
        hmask = P.alloc([128, 256], F32, "hmask")
        P.dma(hmask[:, :], halo_mask[:, :])
        for h in range(8):
            bk = 1 + (h % 2)
            P.mm(pfull(bk, 0, 256), Jm, h2[:, h, :])
            P.tt(biasM[:, h, :], pfull(bk, 0, 256), smask[:, :], ALU.add)
            P.tt(biasM0[:, h, :], biasM[:, h, :], hmask[:, :], ALU.add)

        P.release(mB)
        wq = P.alloc([128, 8, 512], BF16, "wq")
        wkv = P.alloc([128, 8, 256], BF16, "wkv")
        wo = P.alloc([128, 8, D], BF16, "wo")
        for c in range(4):
            for j, hh in enumerate((c, 4 + c)):
                c0 = 2056 + hh * 64
                load_w(wq[:, :, c * 128 + j * 64:c * 128 + (j + 1) * 64], w_in.t[:, c0:c0 + 64], w_in.tok())
        load_w(wkv[:, :, :], w_in.t[:, 2568:2824], w_in.tok())
        load_w(wo[:, :, :], w_out.t[:, :], w_out.tok())
        _hm = P.alloc([128, D], F32, "hm")
        xw = {"junk": _hm, "ssq": P.alloc([128, 1], F32, "ssq"),
              "hm": _hm, "hb": P.alloc([128, D], BF16, "hb"), "tbank": 0}
        xts = [P.alloc([128, D], F32, f"xt{i}") for i in range(3)]
        hTs = [P.alloc([128, 8, 128], BF16, f"hT{i}") for i in range(2)]
        qTt = P.alloc([128, 4, 128], BF16, "qTt")
        kTb = P.alloc([128, 2, 128], BF16, "kTb")
        vb = P.alloc([128, 2, 128], BF16, "vb")
        kvf = P.alloc([128, 256], F32, "kvf")
        sb = [P.alloc([128, 256], F32, f"sb{i}") for i in range(4)]
        pbuf = [P.alloc([128, 256], BF16, f"pb{i}") for i in range(4)]
        pT = [P.alloc([128, 2, 128], BF16, f"pT{i}") for i in range(4)]
        st = P.alloc([128, 8, 4], F32, "st")
        mixes = [P.alloc([128, D], BF16, f"mix{i}") for i in range(2)]
        mix = mixes[0]
        mixT = P.alloc([128, 8, 128], BF16, "mixT")
        xo = [P.alloc([128, D], F32, f"xo{i}") for i in range(2)]

        def proj_swa(hT, need_q=True, qdst=None):
            qdst = qTt if qdst is None else qdst
            if need_q:
                for c in range(4):
                    for kt in range(8):
                        P.mm(pq(0, c), wq[:, kt, c * 128:(c + 1) * 128], hT[:, kt, :], start=(kt == 0), stop=(kt == 7))
                P.act(qdst[:, :, :].re("p a b -> p (a b)"), pfull(0), AF.Identity, scale=0.125)
            for kt in range(8):
                P.mm(pq(2, 0), wkv[:, kt, 0:128], hT[:, kt, :], start=(kt == 0), stop=(kt == 7))
            for kt in range(8):
                P.mm(pfull(2, 256, 512), hT[:, kt, :], wkv[:, kt, :], start=(kt == 0), stop=(kt == 7))

        oix = P.alloc([128, 17 * 4], I32, "oix")
        P.dma(oix[:, :], odn_idx[:, :])
        odn_flat = odn.t.rearrange("j h t c -> (j h t) c")

        def load_odn_gather(j, mx):
            for h in range(4):
                col = j * 4 + h
                P.op("pool", lambda e, h=h, col=col: e.indirect_dma_start(
                    out=mx.t[:, h * 128:(h + 1) * 128], out_offset=None, in_=odn_flat,
                    in_offset=bass.IndirectOffsetOnAxis(ap=oix.t[:, col:col + 1], axis=0)),
                    reads=[odn[0], oix[:, :]], writes=[mx[:, :]], dma=True)

        def out_proj_gen(xt, g, dst_dram, load_odn, mx):
            load_odn()
            yield
            for kt in range(8):
                P.tr(pbf(1)[:, kt * 128:(kt + 1) * 128], mx[:, kt * 128:(kt + 1) * 128], identb)
            yield
            P.copy(mixT[:, :, :].re("p a b -> p (a b)"), pbf(1), eng="act")
            yield
            xoo = xo[0]
            for half in range(2):
                ps = pfull(1)
                for kt in range(8):
                    P.mm(ps, mixT[:, kt, :], wo[:, kt, half * 512:(half + 1) * 512], start=(kt == 0), stop=(kt == 7))
                yield
                hs = slice(half * 512, (half + 1) * 512)
                P.tt(xoo[:, hs], ps, MOD1[g][:, 2, hs], ALU.mult)
                P.tt(xoo[:, hs], xoo[:, hs], xt[:, hs], ALU.add, eng="pool")
                yield
            P.dma(dst_dram, xoo[:, :])

        def out_proj(xt, g, dst_dram, load_odn, mx=None):
            for _ in out_proj_gen(xt, g, dst_dram, load_odn, mix if mx is None else mx):
                pass

        if "tok" in phases:
            qT2 = [qTt, P.alloc([128, 4, 128], BF16, "qTt1")]
            kT3 = P.alloc([128, 3, 128], BF16, "kT3")
            v3 = P.alloc([128, 3, 128], BF16, "v3")

            def stageA_gen(ti):
                xt, hT = xts[ti % 3], hTs[ti % 2]
                P.dma(xt[:, :], xseg[ti * 128:(ti + 1) * 128, :])
                yield
                norm_mod_T(xt[:, :], MOD1[0][:, 1, :], MOD1[0][:, 0, :], hT, xw)
                yield
                qdst = qT2[ti % 2]
                if ti > 0:
                    for c in range(4):
                        for kt in range(8):
                            P.mm(pq(0, c), wq[:, kt, c * 128:(c + 1) * 128], hT[:, kt, :], start=(kt == 0), stop=(kt == 7))
                        yield
                    P.act(qdst[:, :, :].re("p a b -> p (a b)"), pfull(0), AF.Identity, scale=0.125)
                for kt in range(8):
                    P.mm(pq(2, 0), wkv[:, kt, 0:128], hT[:, kt, :], start=(kt == 0), stop=(kt == 7))
                yield
                for kt in range(8):
                    P.mm(pfull(2, 256, 512), hT[:, kt, :], wkv[:, kt, :], start=(kt == 0), stop=(kt == 7))
                yield
                P.copy(kT3[:, ti % 3, :], pq(2, 0), eng="act")
                P.copy(kvf[:, :], pfull(2, 256, 512))
                P.copy(v3[:, ti % 3, :], kvf[:, 128:256], eng="pool")
                if ti == ntiles - 1:
                    P.dma(psk[:, :], kvf[:, 0:128])
                    P.dma(psv[:, :], kvf[:, 128:256])

            def stageA(ti):
                for _ in stageA_gen(ti):
                    pass

            def head_gen(ti, hh, k):
                kh, cq = hh // 4, hh % 4
                r0 = kh * 64
                qT_ = qT2[ti % 2]
                ks = ((ti - 1) % 3, ti % 3)
                bm = biasM0 if ti == 2 else biasM
                bs = bt = 3 + k
                for kb in range(2):
                    P.mm(V(pb[bs].t[:, kb * 128:(kb + 1) * 128], [pb[bs].tok()]), qT_[r0:r0 + 64, cq, :],
                         kT3[r0:r0 + 64, ks[kb], :])
                s_ = sb[k]
                P.tt(s_[:, :], pfull(bs, 0, 256), bm[:, hh, :], ALU.add)
                yield
                P.reduce(st[:, hh, 0:1], s_[:, :], ALU.max)
                P.tt(st[:, hh, 0:1], st[:, hh, 0:1], snk[:, hh:hh + 1], ALU.max)
                P.ts(st[:, hh, 1:2], st[:, hh, 0:1], -1.0, ALU.mult)
                yield
                pp = pbuf[k]
                P.act(pp[:, :], s_[:, :], AF.Exp, bias=st[:, hh, 1:2], accum_out=st[:, hh, 2:3])
                P.act(st[:, hh, 3:4], snk[:, hh:hh + 1], AF.Exp, bias=st[:, hh, 1:2])
                yield
                for kb in range(2):
                    P.tr(pbf(bt)[:, kb * 128:(kb + 1) * 128], pp[:, kb * 128:(kb + 1) * 128], identb)
                P.tt(st[:, hh, 3:4], st[:, hh, 3:4], st[:, hh, 2:3], ALU.add)
                yield
                ptt = pT[k]
                P.copy(ptt[:, :, :].re("p a b -> p (a b)"), pbf(bt)[:, 0:256], eng="act")
                yield
                ov = pb[7].v((slice(0, 128), slice(hh * 64, hh * 64 + 64)), 0)
                for kb in range(2):
                    P.mm(ov, ptt[:, kb, :], v3[:, ks[kb], kh * 64:(kh + 1) * 64], start=(kb == 0), stop=(kb == 1))
                yield

            def stageC(ti):
                j = ti - 1
                mx = mixes[ti % 2]
                return out_proj_gen(xts[ti % 3], 0, xmid[j * 128:(j + 1) * 128, :],
                                    lambda: load_odn_gather(j, mx), mx)

            def stageB(ti):
                qs = [([(lambda k, hh=hh: head_gen(ti, hh, k)) for hh in range(8)], 4)]
                if ti >= 2:
                    qs.append(([lambda k: stageC(ti - 1)], 1))
                if ti + 1 < ntiles:
                    qs.append(([lambda k: stageA_gen(ti + 1)], 1))
                run_multi(qs)
                mx = mixes[ti % 2]
                P.recip(st[:, :, 3], st[:, :, 3])
                P.tt(mx[:, 512:1024].re("p (h c) -> p h c", c=64), pfull(7).re("p (h c) -> p h c", c=64),
                     st[:, :, 3:4].bc([128, 8, 64]), ALU.mult)

            stageA(0)
            if ntiles > 1:
                stageA(1)
            for ti in range(1, ntiles):
                stageB(ti)
            for _ in stageC(ntiles - 1):
                pass
        if "toks" in phases:
            xt, hT = xts[0], hTs[0]
            P.dma(xt[:, :], xsam[:, :])
            norm_mod_T(xt[:, :], MOD1[1][:, 1, :], MOD1[1][:, 0, :], hT, xw)
            proj_swa(hT, need_q=True)
            kTn = P.alloc([128, 128], BF16, "kTn")
            P.copy(kTn[:, :], pq(2, 0), eng="act")
            P.copy(kvf[:, :], pfull(2, 256, 512))
            for b in range(16):
                P.dma(ssk[b, 0:120, :], ck[b, 8:128, :])
                P.dma(ssv[b, 0:120, :], cv[b, 8:128, :])
                P.dma(ssk[b, 120:128, :], kvf[b * 8:(b + 1) * 8, 0:128])
                P.dma(ssv[b, 120:128, :], kvf[b * 8:(b + 1) * 8, 128:256])
            ckb = P.alloc([128, 16, 128], BF16, "ckb")
            cvb = P.alloc([128, 16, 128], BF16, "cvb")
            P.dma(ckb[:, :, :], V(ck.t.rearrange("b k c -> k b c"), [ck.tok()]), eng="pool")
            P.dma(cvb[:, :, :], V(cv.t.rearrange("b k c -> k b c"), [cv.tok()]), eng="pool")
            vnew = P.alloc([8, 16, 128], BF16, "vnew")
            P.dma(vnew[:, :, :], V(ssv.t[:, 120:128, :].rearrange("b t c -> t b c"), [ssv.tok()]), eng="pool")
            kbT = P.alloc([128, 16, 128], BF16, "kbT")
            for half in range(2):
                for bb in range(8):
                    b = half * 8 + bb
                    P.tr(pbf(3)[:, bb * 128:(bb + 1) * 128], ckb[:, b, :], identb)
                P.copy(kbT[:, half * 8:(half + 1) * 8, :].re("p a b -> p (a b)"), pbf(3), eng="act")
            NSL = 2
            ssl = [P.alloc([8, 8, 136], F32, f"ss_{i}") for i in range(NSL)]
            spl = [P.alloc([8, 8, 136], BF16, f"sp_{i}") for i in range(NSL)]
            sttl = [P.alloc([8, 8, 4], F32, f"stt_{i}") for i in range(NSL)]
            pTcl = [P.alloc([128, 8, 8], BF16, f"pTc{i}") for i in range(NSL)]
            pTnl = [P.alloc([8, 8, 8], BF16, f"pTn{i}") for i in range(NSL)]
            o_all = P.alloc([8, 16, 512], BF16, "o_all")

            def seq_gen(b, k):
                ss_, sp_, stt_, pTc, pTn = ssl[k], spl[k], sttl[k], pTcl[k], pTnl[k]
                X, Y, Z = 2 + 3 * k, 3 + 3 * k, 4 + 3 * k
                cs = slice(b * 8, b * 8 + 8)
                for hh in range(8):
                    kh, cq = hh // 4, hh % 4
                    r0 = kh * 64
                    bank = (X, Y)[hh // 4]
                    dst = V(pb[bank].t[0:8, (hh % 4) * 128:(hh % 4 + 1) * 128], [pb[bank].tok()])
                    P.mm(dst, qTt[r0:r0 + 64, cq, cs], kbT[r0:r0 + 64, b, :])
                yield
                for hh in range(8):
                    kh, cq = hh // 4, hh % 4
                    r0 = kh * 64
                    dst = V(pb[Z].t[0:8, hh * 8:hh * 8 + 8], [pb[Z].tok()])
                    P.mm(dst, qTt[r0:r0 + 64, cq, cs], kTn[r0:r0 + 64, cs])
                yield
                for half in range(2):
                    bank = (X, Y)[half]
                    P.tt(ss_[:, half * 4:(half + 1) * 4, 0:128],
                         V(pb[bank].t[0:8, :].rearrange("p (h c) -> p h c", c=128), [pb[bank].tok()]),
                         biasM[0:8, half * 4:(half + 1) * 4, 0:128], ALU.add)
                P.tt(ss_[:, :, 128:136], V(pb[Z].t[0:8, 0:64].rearrange("p (h c) -> p h c", c=8), [pb[Z].tok()]),
                     biasM[0:8, :, 128:136], ALU.add)
                yield
                P.reduce(stt_[:, :, 0], ss_[:, :, :], ALU.max)
                P.tt(stt_[:, :, 0], stt_[:, :, 0], snk[0:8, :], ALU.max)
                P.tt(ss_[:, :, :], ss_[:, :, :], stt_[:, :, 0:1].bc([8, 8, 136]), ALU.subtract)
                yield
                P.act(sp_[:, :, :], ss_[:, :, :], AF.Exp)
                P.reduce(stt_[:, :, 2], sp_[:, :, :], ALU.add)
                P.tt(stt_[:, :, 1], snk[0:8, :], stt_[:, :, 0], ALU.subtract)
                P.act(stt_[:, :, 1], stt_[:, :, 1], AF.Exp)
                yield
                P.tt(stt_[:, :, 3], stt_[:, :, 1], stt_[:, :, 2], ALU.add)
                P.recip(stt_[:, :, 3], stt_[:, :, 3])
                zb = pbf(Z).ap
                for hh in range(8):
                    P.tr(V(zb[:, 256 + hh * 8:256 + hh * 8 + 8], [pb[Z].tok()]), sp_[:, hh, 0:128], identb[0:8, 0:8])
                    P.tr(V(zb[0:8, 512 + hh * 8:512 + hh * 8 + 8], [pb[Z].tok()]), sp_[:, hh, 128:136], identb[0:8, 0:8])
                yield
                P.copy(pTc[:, :, :].re("p a b -> p (a b)"), V(zb[:, 256:320], [pb[Z].tok()]), eng="act")
                P.copy(pTn[:, :, :].re("p a b -> p (a b)"), V(zb[0:8, 512:576], [pb[Z].tok()]), eng="act")
                yield
                for hh in range(8):
                    kh = hh // 4
                    ov = V(pb[X].t[0:8, hh * 64:hh * 64 + 64], [pb[X].tok()])
                    P.mm(ov, pTc[:, hh, :], cvb[:, b, kh * 64:(kh + 1) * 64], start=True, stop=False)
                    P.mm(ov, pTn[:, hh, :], vnew[:, b, kh * 64:(kh + 1) * 64], start=False, stop=True)
                yield
                P.tt(o_all[:, b, :].re("p (h c) -> p h c", c=64),
                     V(pb[X].t[0:8, :].rearrange("p (h c) -> p h c", c=64), [pb[X].tok()]),
                     stt_[:, :, 3:4].bc([8, 8, 64]), ALU.mult)
                yield

            run_window([(lambda k, b=b: seq_gen(b, k)) for b in range(16)], NSL, slotted=True)
            P.dma(V(oswa_s.t.rearrange("b t c -> t b c"), [oswa_s.tok()]), o_all[:, :, :])

            def load_odn_s():
                P.dma(mix[:, 0:512].re("p (h c) -> p h c", c=128), V(odn_s.t.rearrange("h t c -> t h c"), [odn_s.tok()]))
                P.dma(mix[:, 512:1024], V(oswa_s.t.rearrange("b t c -> (b t) c"), [oswa_s.tok()]))
            out_proj(xt, 1, xmid_s[:, :], load_odn_s)
        P.release(mW)
    P.release(mQ)

    if "ffn" in phases or "ffns" in phases:
        MOD2b = P.alloc([128, 3, D], F32, "mod2r")
        wu = P.alloc([128, 8, 5632], BF16, "wu")
        wd = P.alloc([128, 22, D], BF16, "wd")
        for j in (0, 5, 1, 6, 2, 7, 3, 8, 4, 9, 10):
            dst = wu.v((slice(None), slice(None), slice(j * 512, (j + 1) * 512)), ("wu", j))
            P.dma(dst, V(w_up.t[:, j * 512:(j + 1) * 512].rearrange("(kt p) n -> p kt n", p=128), [w_up.tok()]), eng="pool")
        load_w(wd[:, :, :], w_down.t[:, :], w_down.tok())
        _hm = P.alloc([128, D], F32, "hm")
        xw = {"junk": _hm, "ssq": P.alloc([128, 1], F32, "ssq"), "hm": _hm,
              "hb": P.alloc([128, D], BF16, "hb"), "tbank": 0}
        xm = P.alloc([128, D], F32, "xm")
        h2Ts = [P.alloc([128, 8, 256], BF16, f"h2T{i}") for i in range(2)]
        ug = [[P.alloc([128, 260], F32, f"ug{i}_{k}") for k in range(2)] for i in range(2)]
        cc_ = [[P.alloc([128, 256], F32, f"cc{i}_{k}") for k in range(2)] for i in range(2)]
        actT = P.alloc([128, 22, 256], BF16, "actT")
        utok = Buf(actT.t.rearrange("p a b -> p (a b)")[0:32, 0:2816].bitcast(F32), "utok")
        utok.toks = actT.toks
        uh = P.alloc([128, NFC, 2], F32, "uh")
        yo = P.alloc([128, D], F32, "yo")
        ss2 = P.alloc([128, 1], F32, "ss2")

        def ffn_prep(srcs, h2T):
            for i in range(len(srcs)):
                P.dma(xm[:, :], srcs[i])
                norm_mod_T(xm[:, :], MOD2b[:, 1, :], MOD2b[:, 0, :], h2T[:, :, i * 128:(i + 1) * 128], xw)
                yield

        def ffn_compute(srcs, g, mode, dsts, h2T):
            nt = len(srcs)
            NB = nt * 128
            for fc in range(22):
                for k in range(2):
                    fcc = fc + 22 * k
                    ps = pfull(1 + (fc % 2) * 2 + k, 0, NB)
                    for kt in range(8):
                        P.mm(ps, wu.v((slice(None), kt, slice(fcc * 128, (fcc + 1) * 128)), ("wu", fcc // 4)),
                             h2T[:, kt, 0:NB], start=(kt == 0), stop=(kt == 7))
                    cw = fcw[:, fcc, :]
                    if mode == "halo":
                        P.ts(uh[:, fcc, :], ps[:, NB - 2:NB], keep_t[:, 0:1], ALU.mult)
                        continue
                    u_ = ug[fc % 2][k]
                    o_ = cc_[fc % 2][k]
                    if mode == "prompt":
                        P.copy(u_[:, 0:2], uh[:, fcc, :], eng="act")
                        P.copy(u_[:, 2:2 + NB], ps, eng="act")
                        P.copy(uh[:, fcc, :], u_[:, NB:NB + 2], eng="act")
                        taps = [u_[:, j:j + NB] for j in range(3)]
                        ov = o_[:, 0:NB]
                    else:
                        u3 = u_[:, 0:160].re("p (b j) -> p b j", j=10)
                        P.copy(u3[:, :, 0:2], uh_s[:, fcc, :].re("p (b j) -> p b j", j=2), eng="pool")
                        P.copy(u3[:, :, 2:10], ps.re("p (b t) -> p b t", t=8), eng="act")
                        P.copy(uh_s[:, fcc, :].re("p (b j) -> p b j", j=2), u3[:, :, 8:10], eng="pool")
                        taps = [u3[:, :, j:j + 8] for j in range(3)]
                        ov = o_[:, 0:128].re("p (b t) -> p b t", t=8)
                    P.ts(ov, taps[0], cw[:, 0:1], ALU.mult, cw[:, 3:4], ALU.add)
                    P.stt(ov, taps[1], cw[:, 1:2], ov, ALU.mult, ALU.add)
                    P.stt(ov, taps[2], cw[:, 2:3], ov, ALU.mult, ALU.add)
                if mode == "halo":
                    yield
                    continue
                gt_, up_ = cc_[fc % 2]
                P.act(gt_[:, 0:NB], gt_[:, 0:NB], AF.Silu)
                P.tt(actT[:, fc, 0:NB], gt_[:, 0:NB], up_[:, 0:NB], ALU.mult, eng="pool")
                yield
            if mode == "halo":
                return
            for i in range(nt):
                P.dma(xm[:, :], srcs[i])
                for half in range(2):
                    ps = pfull(5 + half)
                    for fc in range(22):
                        P.mm(ps, actT[:, fc, i * 128:(i + 1) * 128], wd[:, fc, half * 512:(half + 1) * 512],
                             start=(fc == 0), stop=(fc == 21))
                    hs = slice(half * 512, (half + 1) * 512)
                    P.tt(yo[:, hs], ps, MOD2b[:, 2, hs], ALU.mult)
                    P.tt(yo[:, hs], yo[:, hs], xm[:, hs], ALU.add, eng="pool")
                P.act(_hm[:, :], yo[:, :], AF.Square, accum_out=ss2[:, :])
                P.ts(ss2[:, :], ss2[:, :], 1.0 / D, ALU.mult, EPS, ALU.add)
                P.act(ss2[:, :], ss2[:, :], AF.Ln)
                P.act(ss2[:, :], ss2[:, :], AF.Exp, scale=-0.5)
                P.stt(yo[:, :], yo[:, :], ss2[:, 0:1], nwf[:, :], ALU.mult, ALU.mult)
                P.dma(dsts[i], yo[:, :])
                yield

        def ffn_block(srcs, g, mode, dsts):
            for _ in ffn_prep(srcs, h2Ts[0]):
                pass
            for _ in ffn_compute(srcs, g, mode, dsts, h2Ts[0]):
                pass

        def u_tokmajor(src3, nrow, dst_dram):
            for grp in range(4):
                for i in range(11):
                    fcc = grp * 11 + i
                    bank = 1 + i % 2
                    dst = V(pb[bank].t[0:nrow, 0:128], [pb[bank].tok()])
                    P.mm(dst, src3(fcc), ident)
                    P.copy(utok[0:nrow, i * 128:(i + 1) * 128], dst)
                P.dma(dst_dram[:, grp * 1408:(grp + 1) * 1408], utok[0:nrow, :])

        if "ffn" in phases:
            P.dma(MOD2b[:, :, :], mod2_d[0])
            blocks = [([xmid[0:128, :]], "halo", None)]
            for j in range(1, ntiles - 1, 2):
                js = [jj for jj in (j, j + 1) if jj < ntiles - 1]
                blocks.append(([xmid[jj * 128:(jj + 1) * 128, :] for jj in js], "prompt",
                               [y_seg[(jj - 1) * 128:jj * 128, :] for jj in js]))
            for _ in ffn_prep(blocks[0][0], h2Ts[0]):
                pass
            for bi, (srcs_, mode_, dsts_) in enumerate(blocks):
                qs = [([lambda k, a=(srcs_, 0, mode_, dsts_, h2Ts[bi % 2]): ffn_compute(*a)], 1)]
                if bi + 1 < len(blocks):
                    qs.append(([lambda k, a=(blocks[bi + 1][0], h2Ts[(bi + 1) % 2]): ffn_prep(*a)], 1))
                run_multi(qs)
            u_tokmajor(lambda fcc: uh[:, fcc, :], 2, pfc)
        if "ffns" in phases:
            uh_s = P.alloc([128, NFC, 32], F32, "uh_s")
            for grp in range(4):
                P.dma(utok[0:32, :], st_ffn[:, grp * 1408:(grp + 1) * 1408])
                for i in range(11):
                    fcc = grp * 11 + i
                    bank = 1 + fcc % 2
                    dst = V(pb[bank].t[:, 0:32], [pb[bank].tok()])
                    P.mm(dst, utok[0:32, i * 128:(i + 1) * 128], ident[0:32, 0:32])
                    P.copy(uh_s[:, fcc, :], dst)
            P.dma(MOD2b[:, :, :], mod2_d[1])
            ffn_block([xmid_s[:, :]], 1, "sample", [y_s[:, :]])
            sfc2 = Buf(sfc.t.rearrange("b j c -> (b j) c"), "sfc2")
            u_tokmajor(lambda fcc: uh_s[:, fcc, :], 32, sfc2)
    P.emit()
    P.close()
    return nc


def _bucket(d):
    d = np.maximum(d, 0)
    logv = np.log(np.maximum(d, 1).astype(np.float32) / np.float32(16)) / np.float32(math.log(8.0))
    large = np.minimum(16 + (logv * 16).astype(np.int32), 31)
    return np.where(d < 16, d, large)


def _consts():
    i = np.arange(128)
    ident = np.eye(128, dtype=np.float32)
    maskSL = (i[:, None] > i[None, :]).astype(np.float32)
    maskUI = (i[:, None] <= i[None, :]).astype(np.float32)
    J = ident[::-1].copy()
    ones = np.ones((128, 128), np.float32)
    big = np.float32(30000.0)
    cst = np.stack([ident, maskSL, maskUI, J, ones, big * (1 - maskSL), -big * (1 - maskUI)], axis=1)
    e = np.arange(384)
    bk = _bucket(255 - e)
    oh = (bk[None, :] == np.arange(32)[:, None]).astype(np.float32)
    c = np.arange(256)
    dist = i[:, None] + 128 - c[None, :]
    swa_mask = np.where((dist >= 0) & (dist < 128), 0.0, NEG).astype(np.float32)
    return cst, oh, swa_mask


def _odn_idx(s):
    t = np.clip(s * 2048 - 128 + np.arange(17)[None, :, None] * 128 + np.arange(128)[:, None, None], 0, SEQ - 1)
    h = np.arange(4)[None, None, :]
    return ((t // 2048) * 8192 + h * 2048 + t % 2048).astype(np.int32).reshape(128, 68)


_NC_CACHE = {}


def kernel(x_prompt, x_sample, state_dn_conv, state_dn_ssm, cache_swa_k, cache_swa_v, state_ffn_conv,
           c_prompt, c_sample, rel_bias, final_norm_w, w_ada, b_ada, norm_mix_w, w_in, dn_conv_w,
           dn_A_log, dn_dt_bias, dn_norm_w, swa_sinks, w_out, norm_ffn_w, ffn_w_up, ffn_conv_w,
           ffn_conv_b, ffn_w_down, _phases=None, _nchunks=64, _ntiles=NT_SEG):
    f32 = lambda a: np.ascontiguousarray(np.asarray(a, dtype=np.float32))
    phases = _phases if _phases is not None else ("dn", "dns", "tok", "toks", "ffn", "ffns")
    nc = build(phases, _nchunks, _ntiles)
    cst, oh, swa_mask = _consts()
    xp = f32(x_prompt)
    xs = f32(x_sample)
    normw = np.stack([f32(norm_mix_w)[0], f32(norm_ffn_w)[0], f32(final_norm_w)])
    dn_convT = f32(dn_conv_w)[0].reshape(4, 12, 128).transpose(2, 1, 0).copy()
    fcT = np.concatenate([f32(ffn_conv_w)[0], f32(ffn_conv_b)], axis=0).reshape(4, NFC, 128).transpose(2, 1, 0).copy()
    dn_small = np.concatenate([f32(dn_A_log)[0], f32(dn_dt_bias)[0], f32(dn_norm_w)[0]])[None, :]
    shared = {
        "w_ada": f32(w_ada)[0], "b_ada": f32(b_ada), "w_in": f32(w_in)[0], "w_out": f32(w_out)[0],
        "w_up": f32(ffn_w_up)[0], "w_down": f32(ffn_w_down)[0], "normw": normw, "dn_convT": dn_convT,
        "ffn_convT": fcT, "dn_small": dn_small, "sinks": f32(swa_sinks), "rel_bias": f32(rel_bias),
        "cst": cst, "oh": oh, "swa_mask": swa_mask,
    }
    in_maps = []
    for core in range(8):
        b, s = core // 4, core % 4
        seg = np.zeros((NT_SEG * 128, D), np.float32)
        lo = s * 2048 - 256
        if lo < 0:
            seg[256:] = xp[b, 0:2048]
        else:
            seg[:] = xp[b, lo:lo + NT_SEG * 128]
        sb = slice(core * 16, core * 16 + 16)
        hm = np.zeros((128, 256), np.float32)
        if s == 0:
            hm[:, :128] = NEG
        m = dict(shared)
        m.update({
            "xfull": xp[b], "xseg": seg, "xsam": xs[sb].reshape(128, D),
            "cexp": np.stack([np.repeat(f32(c_prompt)[b:b + 1], 128, axis=0),
                              np.repeat(f32(c_sample)[sb], 8, axis=0)]),
            "halo_mask": hm, "keep": np.full((128, 1), 0.0 if s == 0 else 1.0, np.float32),
            "st_conv": f32(state_dn_conv)[0, sb].reshape(48, 1536),
            "st_ssm": f32(state_dn_ssm)[0, sb],
            "ck": f32(cache_swa_k)[0, sb].reshape(16, 128, 128),
            "cv": f32(cache_swa_v)[0, sb].reshape(16, 128, 128),
            "st_ffn": f32(state_ffn_conv)[0, sb].reshape(32, 5632),
            "w_dnp": np.ascontiguousarray(np.concatenate(
                [f32(w_in)[0][:, g * 512 + s * 128:g * 512 + (s + 1) * 128] for g in range(4)]
                + [f32(w_in)[0][:, 2048 + s:2049 + s], f32(w_in)[0][:, 2052 + s:2053 + s]], axis=1)),
            "dcw_p": np.ascontiguousarray(dn_convT[:, [s, 4 + s, 8 + s], :]),
            "dsm_p": np.array([[f32(dn_A_log)[0, s], f32(dn_dt_bias)[0, s]]], np.float32),
            "odn_idx": _odn_idx(s),
        })
        in_maps.append(m)
    import os
    _tr = os.environ.get('DBG_TRACE') == '1'
    res = run_bass_kernel_spmd(nc, in_maps, core_ids=list(range(8)), **({'trace': True} if _tr else {}))
    if _tr:
        print('EXEC_NS', res.exec_time_ns, flush=True)
    R = res.results
    y_prompt = np.stack([np.concatenate([R[b * 4 + s]["y_seg"] for s in range(4)], axis=0) for b in range(2)])
    y_sample = np.concatenate([R[c]["y_s"].reshape(16, 8, D) for c in range(8)], axis=0)
    p_dn_conv = np.stack([np.stack([R[b * 4 + s]["pdc"] for s in range(4)], axis=2).reshape(3, 1536)
                          for b in range(2)])[None]
    s_dn_conv = np.concatenate([R[c]["sdc"] for c in range(8)], axis=0)[None]
    p_dn_ssm = np.stack([np.stack([R[b * 4 + s]["pss"] for s in range(4)]) for b in range(2)])[None]
    s_dn_ssm = np.concatenate([R[c]["sss"] for c in range(8)], axis=0)[None]
    p_swa_k = np.stack([R[b * 4 + 3]["psk"].reshape(128, 2, 64) for b in range(2)])[None]
    s_swa_k = np.concatenate([R[c]["ssk"].reshape(16, 128, 2, 64) for c in range(8)], axis=0)[None]
    p_swa_v = np.stack([R[b * 4 + 3]["psv"].reshape(128, 2, 64) for b in range(2)])[None]
    s_swa_v = np.concatenate([R[c]["ssv"].reshape(16, 128, 2, 64) for c in range(8)], axis=0)[None]
    p_ffn_conv = np.stack([R[b * 4 + 3]["pfc"] for b in range(2)])[None]
    s_ffn_conv = np.concatenate([R[c]["sfc"] for c in range(8)], axis=0)[None]
    outs = (y_prompt, y_sample, p_dn_conv, s_dn_conv, p_dn_ssm, s_dn_ssm, p_swa_k, s_swa_k,
            p_swa_v, s_swa_v, p_ffn_conv, s_ffn_conv)
    return tuple(np.ascontiguousarray(o, dtype=np.float32) for o in outs)
```
